# Optimizing a Trainium2 kernel written in Bass

```python
import jax, jax.numpy as jnp
from jax import lax
import numpy as np

D_MODEL = 1024
BATCH = 8
SEQ = 8192
DEPTH = 1

EXPAND = 2
D_MIX = EXPAND * D_MODEL
C_CONV = D_MIX // 2
C_SGU = D_MIX - C_CONV
CONV_GROUPS = 8
SGU_HEADS = 8
SGU_HEAD_DIM = C_SGU // SGU_HEADS
CHUNK = 128
CONV_WIDTH = 31
CONV_PAD = CONV_WIDTH // 2
D_IN = 3 * C_CONV + 3 * C_SGU
EPS = 1e-6

kernel_name = "hybrid_conformer_conv_chunked_sgu_block"


def rms_norm(x, g):
    xf = x.astype(jnp.float32)
    y = xf * lax.rsqrt(jnp.mean(xf * xf, axis=-1, keepdims=True) + EPS)
    return (y * g.astype(jnp.float32)).astype(x.dtype)


def layer_norm(x, g, b):
    xf = x.astype(jnp.float32)
    mu = jnp.mean(xf, axis=-1, keepdims=True)
    xc = xf - mu
    var = jnp.mean(xc * xc, axis=-1, keepdims=True)
    y = xc * lax.rsqrt(var + EPS)
    return (y * g.astype(jnp.float32) + b.astype(jnp.float32)).astype(x.dtype)


def depthwise_conv(x, w, b):
    y = lax.conv_general_dilated(
        x, w[:, None, :].astype(x.dtype), window_strides=(1,),
        padding=[(CONV_PAD, CONV_PAD)],
        dimension_numbers=("NWC", "WIO", "NWC"),
        feature_group_count=x.shape[-1])
    return y + b.astype(x.dtype)


def conformer_conv_branch(a_val, a_gate, conv_w, conv_b, ln_g, ln_b):
    h = a_val * jax.nn.sigmoid(a_gate)
    h = depthwise_conv(h, conv_w, conv_b)
    h = layer_norm(h, ln_g, ln_b)
    return jax.nn.silu(h)


def chunked_sgu_branch(u, v, ln_g, ln_b, w_s, b_s):
    bsz, seq, _ = v.shape
    n_chunks = seq // CHUNK
    v = layer_norm(v, ln_g, ln_b)
    v = v.reshape(bsz, n_chunks, CHUNK, SGU_HEADS, SGU_HEAD_DIM)
    mixed = jnp.einsum("hpq,bcqhd->bcphd", w_s.astype(v.dtype), v)
    mixed = mixed + jnp.transpose(b_s).astype(v.dtype)[None, None, :, :, None]
    return u * mixed.reshape(bsz, seq, C_SGU)


def setup_inputs(seed: int = 0) -> dict:
    key = jax.random.key(seed)
    ks = jax.random.split(key, 16)
    f32 = jnp.float32
    x = jax.random.normal(ks[0], (BATCH, SEQ, D_MODEL), f32)
    norm_g = 1.0 + 0.02 * jax.random.normal(ks[1], (DEPTH, D_MODEL), f32)
    w_in = jax.random.normal(ks[2], (DEPTH, D_MODEL, D_IN), f32) * D_MODEL ** -0.5
    conv_w = jax.random.normal(ks[3], (DEPTH, CONV_WIDTH, C_CONV), f32) * CONV_WIDTH ** -0.5
    conv_b = 0.02 * jax.random.normal(ks[4], (DEPTH, C_CONV), f32)
    conv_ln_g = 1.0 + 0.02 * jax.random.normal(ks[5], (DEPTH, C_CONV), f32)
    conv_ln_b = 0.02 * jax.random.normal(ks[6], (DEPTH, C_CONV), f32)
    sgu_ln_g = 1.0 + 0.02 * jax.random.normal(ks[7], (DEPTH, C_SGU), f32)
    sgu_ln_b = 0.02 * jax.random.normal(ks[8], (DEPTH, C_SGU), f32)
    w_s = jax.random.normal(ks[9], (DEPTH, SGU_HEADS, CHUNK, CHUNK), f32) * CHUNK ** -0.5
    b_s = 1.0 + 0.02 * jax.random.normal(ks[10], (DEPTH, SGU_HEADS, CHUNK), f32)
    w_out = jax.random.normal(ks[11], (DEPTH, D_MIX, D_MODEL), f32) * D_MIX ** -0.5
    final_g = 1.0 + 0.02 * jax.random.normal(ks[12], (D_MODEL,), f32)
    return {"x": x, "norm_g": norm_g, "w_in": w_in, "conv_w": conv_w, "conv_b": conv_b,
            "conv_ln_g": conv_ln_g, "conv_ln_b": conv_ln_b, "sgu_ln_g": sgu_ln_g,
            "sgu_ln_b": sgu_ln_b, "w_s": w_s, "b_s": b_s, "w_out": w_out, "final_g": final_g}


def reference(x, norm_g, w_in, conv_w, conv_b, conv_ln_g, conv_ln_b, sgu_ln_g,
              sgu_ln_b, w_s, b_s, w_out, final_g):
    split_points = [C_CONV, 2 * C_CONV, 3 * C_CONV,
                    3 * C_CONV + C_SGU, 3 * C_CONV + 2 * C_SGU]
    for l in range(DEPTH):
        h = rms_norm(x, norm_g[l])
        proj = jnp.einsum("bsd,de->bse", h, w_in[l].astype(h.dtype))
        a_val, a_gate, g_conv, u, v, g_sgu = jnp.split(proj, split_points, axis=-1)
        y_conv = conformer_conv_branch(a_val, a_gate, conv_w[l], conv_b[l],
                                       conv_ln_g[l], conv_ln_b[l]) * jax.nn.silu(g_conv)
        y_sgu = chunked_sgu_branch(u, v, sgu_ln_g[l], sgu_ln_b[l],
                                   w_s[l], b_s[l]) * jax.nn.silu(g_sgu)
        y = jnp.concatenate([y_conv, y_sgu], axis=-1)
        x = x + jnp.einsum("bse,ed->bsd", y, w_out[l].astype(y.dtype))
    return rms_norm(x, final_g)
```

```python
import numpy as np
import concourse.bass as bass
import concourse.mybir as mybir
from concourse.bass_utils import run_bass_kernel_spmd
from contextlib import ExitStack

F32 = mybir.dt.float32
BF16 = mybir.dt.bfloat16
AF = mybir.ActivationFunctionType
ALU = mybir.AluOpType


class Sched:
    ENG = ("pe", "act", "dve", "pool", "sp")

    def __init__(self, nc, es):
        self.nc = nc
        self.es = es
        self.streams = {e: [] for e in self.ENG}
        self.sems = {}
        self.count = {}
        for e in ("pe", "act", "dve", "pool"):
            self.sems[e] = es.enter_context(nc.semaphore("prog_" + e))
            self.count[e] = 0
        self.seen = {e: {} for e in self.ENG}
        self.bufs = {}
        self.pending = {e: False for e in self.ENG}

    def dma_sem(self, name):
        s = self.es.enter_context(self.nc.semaphore("dma_" + name))
        self.sems[name] = s
        self.count[name] = 0
        return name

    def _deps(self, eng, reads, writes):
        deps = {}

        def add(clk):
            if clk is None:
                return
            k, v = clk
            if k == "pe" and eng == "pe":
                return
            if v > self.count[k]:
                raise RuntimeError(f"dependency on un-milestoned op {k}:{v} > {self.count[k]}")
            if deps.get(k, 0) < v:
                deps[k] = v

        for b in reads:
            st = self.bufs.get(b)
            if st is not None:
                add(st["w"])
        for b in writes:
            st = self.bufs.get(b)
            if st is not None:
                add(st["w"])
                for r in st["r"]:
                    add(r)
        out = []
        for k, v in deps.items():
            if self.seen[eng].get(k, 0) < v:
                self.seen[eng][k] = v
                out.append((k, v))
        return out

    def _mark(self, clk, reads, writes):
        for b in reads:
            st = self.bufs.setdefault(b, {"w": None, "r": []})
            st["r"].append(clk)
        for b in writes:
            self.bufs[b] = {"w": clk, "r": []}

    def op(self, eng, fn, reads=(), writes=(), inc=True):
        waits = self._deps(eng, reads, writes)
        clk = (eng, self.count[eng] + 1)
        if inc:
            self.count[eng] += 1
            self.pending[eng] = False
        else:
            self.pending[eng] = True
        self._mark(clk, reads, writes)
        self.streams[eng].append((waits, fn, (eng, 1) if inc else None))

    def dma(self, queue, sem, fn, reads=(), writes=()):
        waits = self._deps(queue, reads, writes)
        self.count[sem] += 16
        clk = (sem, self.count[sem])
        self._mark(clk, reads, writes)
        self.streams[queue].append((waits, fn, (sem, 16)))

    def finish(self, queues, sems):
        for q in queues:
            waits = []
            for s in sems:
                waits.append((s, self.count[s]))
            self.streams[q].append((waits, None, None))

    def emit(self):
        nc = self.nc
        for e in ("pe",):
            assert not self.pending[e], "trailing un-milestoned PE ops"
        sems = self.sems
        streams = self.streams

        def replay(name):
            def run(eng):
                for waits, fn, inc in streams[name]:
                    for k, v in waits:
                        eng.wait_ge(sems[k], v)
                    if fn is None:
                        continue
                    ins = fn(eng)
                    if inc is not None:
                        ins.then_inc(sems[inc[0]], inc[1])
            return run

        with nc.Block() as block:
            block.tensor(replay("pe"))
            block.scalar(replay("act"))
            block.vector(replay("dve"))
            block.gpsimd(replay("pool"))
            block.sync(replay("sp"))


D = 1024
DIN = 6144
T = 512
NSUB = 4
KW = 31
NSLOT = 4
NDVE = 6
STATS_LAG = 3
EPS = 1e-6


def build_program(S_TOK):
    NT = S_TOK // T
    nc = bass.Bass("TRN2", target_bir_lowering=False)

    def din(name, shape):
        return nc.dram_tensor(name, shape, F32, kind="ExternalInput").ap()

    x_d = din("x", [S_TOK, D])
    w_in_d = din("w_in", [D, DIN])
    w_out_d = din("w_out", [2 * D, D])
    conv_w_d = din("conv_w", [KW, D])
    conv_b_d = din("conv_b", [1, D])
    cln_g_d = din("conv_ln_g", [1, D])
    cln_b_d = din("conv_ln_b", [1, D])
    sln_g_d = din("sgu_ln_g", [1, D])
    sln_b_d = din("sgu_ln_b", [1, D])
    norm_g_d = din("norm_g", [1, D])
    final_g_d = din("final_g", [1, D])
    w_s_d = din("w_s", [8, 128, 128])
    b_s_d = din("b_s", [1, D])
    out_d = nc.dram_tensor("out", [S_TOK, D], F32, kind="ExternalOutput").ap()
    win_s = nc.dram_tensor("win_s", [12, 128, 4096], BF16, kind="Internal").ap()
    diag_s = nc.dram_tensor("diag_s", [8, 128, KW * 128], BF16, kind="Internal").ap()

    with ExitStack() as es:
        def sb(name, shape, dt):
            return es.enter_context(nc.sbuf_tensor(name, shape, dt))

        ring = sb("ring", [128, NSLOT, 4096], BF16)
        wout = sb("wout", [128, 16, 1024], BF16)
        hT = sb("hT", [128, 2, 8, 512], BF16)
        hg = sb("hg", [128, 2, 8, 544], BF16)
        sgc = sb("sgc", [128, 8, 512], BF16)
        cb = sb("cb", [128, 8, 512], BF16)
        ycv = sb("ycv", [128, 8, 512], BF16)
        ysg = sb("ysg", [128, 8, 512], BF16)
        xin = sb("xin", [128, 2, 1024], F32)
        xs = sb("xs", [128, 4, 1024], BF16)
        junk = sb("junk", [128, 2, 1024], BF16)
        jstate = {"n": 0}

        def next_junk():
            j = jstate["n"] % 2
            jstate["n"] += 1
            return j
        th = sb("th", [128, 2, 512], BF16)
        sgs = sb("sgs", [128, 2, 512], BF16)
        sq = sb("sq", [128, 4, 512], BF16)
        mean_sb = sb("mean_sb", [128, 512], F32)
        rstd_sb = sb("rstd_sb", [128, 512], F32)
        nmr_sb = sb("nmr_sb", [128, 512], F32)
        t1 = sb("t1", [128, 2, 512], F32)
        sw = sb("sw", [128, 2, 512], BF16)
        vg = sb("vg", [128, 2, 1024], BF16)
        wst = sb("wst", [128, 1024], BF16)
        xres = sb("xres", [128, 3, 1024], F32)
        Gfin = sb("Gfin", [128, 1024], F32)
        Gv = sb("Gv", [128, 1024], F32)
        Gn = sb("Gn", [128, 1024], F32)
        colp = sb("colp", [128, 8, 35], F32)
        colh = sb("colh", [128, 8, 31], F32)
        ident = sb("ident", [128, 128], BF16)
        identf = sb("identf", [128, 128], F32)
        onesm = sb("onesm", [128, 128], BF16)
        onec = sb("onec", [128, 1], BF16)
        lsm = sb("lsm", [128, 8, 128], BF16)
        rsm = sb("rsm", [128, 8, 128], BF16)
        nh = sb("nh", [128, 1], F32)
        st = sb("st", [128, 32], F32)

        pp = [es.enter_context(nc.psum_tensor("pp%d" % j, [128, 1024], F32)) for j in range(4)]

        def bank(b):
            return pp[b // 2][:, (b % 2) * 512:(b % 2) * 512 + 512]

        def PS(b):
            return ("ps", b)

        S = Sched(nc, es)
        ring_sem = [S.dma_sem("ring%d" % i) for i in range(NSLOT)]
        ringc_sem = [S.dma_sem("ringc%d" % i) for i in range(NSLOT)]
        xin_sem = [S.dma_sem("xin%d" % i) for i in range(2)]
        xinh_sem = [S.dma_sem("xinh%d" % i) for i in range(2)]
        xres_sem = [S.dma_sem("xres%d" % i) for i in range(3)]
        out_sem = [S.dma_sem("out%d" % i) for i in range(3)]
        pro_sem = S.dma_sem("pro")
        pro2_sem = S.dma_sem("pro2")

        bstate = {"next": 0}

        def alloc_bank():
            b = bstate["next"]
            bstate["next"] = (b + 1) % 8
            return b

        def alloc_pair():
            b = bstate["next"]
            if b % 2:
                b = (b + 1) % 8
            bstate["next"] = (b + 2) % 8
            return b

        def mk_ident(t, key):
            S.op("pool", lambda e: e.memset(t[:], 0.0), writes=[key])
            S.op("pool", lambda e: e.affine_select(out=t[:], in_=t[:], compare_op=ALU.not_equal, fill=1.0,
                                                    base=0, pattern=[[-1, 128]], channel_multiplier=1),
                 reads=[key], writes=[key])

        mk_ident(ident, "ident")
        mk_ident(identf, "identf")
        S.op("pool", lambda e: e.memset(onesm[:], 1.0 / 1024), writes=["onesm"])
        S.op("pool", lambda e: e.memset(onec[:], 1.0), writes=["onec"])
        S.op("pool", lambda e: e.memset(nh[:], -0.5), writes=["nh"])
        S.op("pool", lambda e: e.memset(lsm[:], 0.0), writes=["lsm"])
        S.op("pool", lambda e: e.memset(lsm[32:33, :, :], 1.0), reads=["lsm"], writes=["lsm"])
        S.op("pool", lambda e: e.memset(lsm[64:65, :, :], 1.0), reads=["lsm"], writes=["lsm"])
        S.op("pool", lambda e: e.memset(rsm[:], 0.0), writes=["rsm0", "rsm1", "rsm2"])
        S.op("pool", lambda e: e.memset(hg[:, 0, :, 0:16], 0.0), writes=[("hgl", 0)])

        RSM = ["rsm0", "rsm1", "rsm2"]

        def prologue_params():
            def pdma(queue, fn, reads=(), writes=(), sem=None):
                S.dma(queue, sem or pro_sem, fn, reads=reads, writes=writes)

            bsems = [S.dma_sem("b%d" % i) for i in range(6)]
            S.dma("sp", bsems[0], lambda e: e.dma_start(out=Gfin[:], in_=final_g_d[0:1, :].to_broadcast([128, 1024])), writes=["Gfin"])
            S.dma("sp", bsems[1], lambda e: e.dma_start(out=Gv[:], in_=sln_g_d[0:1, :].to_broadcast([128, 1024])), writes=["Gv"])

            R = xres[:, 0, :]
            rsems = [S.dma_sem("r%d" % i) for i in range(4)]
            S.dma("sp", rsems[0], lambda e: e.dma_start(out=R[0:31, :], in_=conv_w_d[:, :]), writes=[("xres", 0)])
            S.dma("sp", rsems[1], lambda e: e.dma_start(out=R[31:32, :], in_=conv_b_d[:, :]), writes=[("xres", 0)])
            S.dma("sp", rsems[2], lambda e: e.dma_start(out=R[32:33, :], in_=cln_g_d[:, :]), writes=[("xres", 0)])
            S.dma("sp", rsems[3], lambda e: e.dma_start(out=R[33:34, :], in_=cln_b_d[:, :]), writes=[("xres", 0)])
            for c in range(8):
                b = alloc_bank()
                S.op("pe", lambda e, b=b, c=c: e.matmul(bank(b)[:, 0:34], lhsT=R[0:34, c * 128:(c + 1) * 128],
                                                        rhs=identf[0:34, 0:34], start=True, stop=True),
                     reads=[("xres", 0), "identf"], writes=[PS(b)])
                S.op("dve", lambda e, b=b, c=c: e.tensor_copy(out=colp[:, c, 0:34], in_=bank(b)[:, 0:34]),
                     reads=[PS(b)], writes=[("colp", c)])
                S.op("dve", lambda e, c=c: e.tensor_scalar(out=colh[:, c, :], in0=colp[:, c, 0:31], scalar1=0.5, scalar2=None, op0=ALU.mult),
                     reads=[("colp", c)], writes=[("colp", c)])
            COLP = [("colp", c) for c in range(8)]

            wsp = vg
            S.dma("pool", pro_sem, lambda e: e.dma_start(
                out=vg[:, 0, :].rearrange("p (h q) -> p h q", h=8), in_=w_s_d.rearrange("h p q -> p h q")), writes=[("vg", 0)])
            b = alloc_bank()
            tpv = bank(b).bitcast(BF16).rearrange("p (a c) -> p a c", a=8)
            for h in range(8):
                S.op("pe", lambda e, h=h: e.transpose(out=tpv[:, h, :], in_=vg[:, 0, h * 128:(h + 1) * 128], identity=ident[:]),
                     reads=[("vg", 0), "ident"], writes=[PS(b)], inc=(h == 7))
            S.op("dve", lambda e: e.tensor_copy(out=wst[:].rearrange("p (a c) -> p a c", a=8), in_=tpv), reads=[PS(b)], writes=["wst"])
            b2 = alloc_pair()
            for half in range(2):
                S.op("pe", lambda e, half=half: e.matmul(bank(b2 + half)[0:1, :], lhsT=onec[:, 0:1],
                                                         rhs=wst[:, half * 512:(half + 1) * 512], start=True, stop=True),
                     reads=["wst", "onec"], writes=[PS(b2 + half)])
            S.op("dve", lambda e: e.tensor_copy(out=rsm[0:1, :, :].rearrange("o h p -> o (h p)"), in_=pp[b2 // 2][0:1, :]),
                 reads=[PS(b2), PS(b2 + 1)], writes=["rsm0"])
            r0 = xres[0:1, 1, :]
            r1 = t1[0:1, :, :].rearrange("o a n -> o (a n)")
            rb0 = sgs[0:1, :, :].rearrange("o a n -> o (a n)")
            rb1 = sw[0:1, :, :].rearrange("o a n -> o (a n)")
            S.dma("sp", pro2_sem, lambda e: e.dma_start(out=r0, in_=b_s_d[:, :]), writes=[("xres", 1)])
            S.op("dve", lambda e: e.tensor_copy(out=rb0, in_=r0), reads=[("xres", 1)], writes=[("sgs", 0), ("sgs", 1)])
            S.op("dve", lambda e: e.tensor_copy(out=r1, in_=rb0), reads=[("sgs", 0), ("sgs", 1)], writes=[("t1", 0), ("t1", 1)])
            S.op("dve", lambda e: e.tensor_tensor(out=rb1, in0=r0, in1=r1, op=ALU.subtract),
                 reads=[("xres", 1), ("t1", 0), ("t1", 1)], writes=[("sw", 0), ("sw", 1)])
            S.dma("sp", bsems[3], lambda e: e.dma_start(out=rsm[32:33, :, :].rearrange("o h p -> o (h p)"), in_=rb0),
                  reads=[("sgs", 0), ("sgs", 1)], writes=["rsm1"])
            S.dma("sp", bsems[4], lambda e: e.dma_start(out=rsm[64:65, :, :].rearrange("o h p -> o (h p)"), in_=rb1),
                  reads=[("sw", 0), ("sw", 1)], writes=["rsm2"])
            S.dma("pool", bsems[5], lambda e: e.dma_start(out=lsm[0:1, :, :].rearrange("o h p -> o (h p)"), in_=sln_b_d[:, :]),
                  reads=["lsm"], writes=["lsm"])

        pinned = set()

        def alloc_bank():
            while True:
                b = bstate["next"]
                bstate["next"] = (b + 1) % 8
                if b not in pinned:
                    return b

        def alloc_pair():
            while True:
                b = bstate["next"]
                if b % 2:
                    b = (b + 1) % 8
                bstate["next"] = (b + 2) % 8
                if b not in pinned and (b + 1) not in pinned:
                    return b

        stream = []
        A_PAGES = [("w", 0), ("w", 2), ("w", 1), ("w", 3)]
        CS_PAGES = [("w", 6), ("w", 10), ("d", 0), ("d", 1), ("d", 2), ("d", 3), ("d", 4), ("w", 7), ("w", 11),
                    ("d", 5), ("d", 6), ("d", 7), ("w", 4), ("w", 5), ("w", 8), ("w", 9)]
        DIAG_OFF = {0: 2, 1: 3, 2: 4, 3: 5, 4: 6, 5: 9, 6: 10, 7: 11}
        stream += A_PAGES
        if NT > 1:
            stream += A_PAGES
        for i in range(NT):
            stream += CS_PAGES
            if i + 2 < NT:
                stream += A_PAGES
        ws = {"issued": 0, "free": list(range(NSLOT)), "slot_of": {}, "cur": 0}

        done_w = set()
        done_d = set()

        def prefetch():
            while ws["issued"] < len(stream) and ws["free"]:
                q = ws["issued"]
                kind, idx = stream[q]
                sl = ws["free"].pop(0)
                ws["slot_of"][q] = sl
                if kind == "w":
                    if idx not in done_w:
                        done_w.add(idx)
                        S.dma("pool", ringc_sem[sl], lambda e, idx=idx, sl=sl: e.dma_start(
                            out=ring[:, sl, :].rearrange("p (c n) -> p c n", c=8),
                            in_=w_in_d[:, idx * 512:(idx + 1) * 512].rearrange("(c p) n -> p c n", p=128)), writes=[("ring", sl)])
                        S.dma("sp", ring_sem[sl], lambda e, idx=idx, sl=sl: e.dma_start(out=win_s[idx], in_=ring[:, sl, :]),
                              reads=[("ring", sl)], writes=[("win_s", idx)])
                    else:
                        S.dma("sp", ring_sem[sl], lambda e, idx=idx, sl=sl: e.dma_start(out=ring[:, sl, :], in_=win_s[idx]),
                              reads=[("win_s", idx)], writes=[("ring", sl)])
                else:
                    if idx not in done_d:
                        done_d.add(idx)
                        S.op("dve", lambda e, idx=idx, sl=sl: e.scalar_tensor_tensor(
                            out=ring[:, sl, 0:KW * 128].rearrange("p (k c) -> p k c", k=KW),
                            in0=ident[:].unsqueeze(1).to_broadcast([128, KW, 128]), scalar=0.5,
                            in1=colp[:, idx, 0:KW].unsqueeze(2).to_broadcast([128, KW, 128]),
                            op0=ALU.mult, op1=ALU.mult),
                            reads=["ident", ("colp", idx)], writes=[("ring", sl)])
                        S.dma("sp", ring_sem[sl], lambda e, idx=idx, sl=sl: e.dma_start(out=diag_s[idx], in_=ring[:, sl, 0:KW * 128]),
                              reads=[("ring", sl)], writes=[("diag_s", idx)])
                    else:
                        S.dma("sp", ring_sem[sl], lambda e, idx=idx, sl=sl: e.dma_start(out=ring[:, sl, 0:KW * 128], in_=diag_s[idx]),
                              reads=[("diag_s", idx)], writes=[("ring", sl)])
                ws["issued"] += 1

        def release(*seqs):
            for q in seqs:
                ws["free"].append(ws["slot_of"].pop(q))
            prefetch()

        def wslot(seq):
            assert seq in ws["slot_of"], "page %d not resident (ring deadlock)" % seq
            return ws["slot_of"][seq]

        def wpage(seq):
            sl = wslot(seq)
            return ring[:, sl, :].rearrange("p (c n) -> p c n", c=8), ("ring", sl)

        HT = lambda hb: [("hT", hb, s) for s in range(NSUB)]

        def P_front(i, s, hw=False, q="sp"):
            P_load(i, s, hw, q)
            P_chain(i, s)

        def P_load(i, s, hw=False, q="sp"):
            xb = s % 2
            tok0 = i * T + s * 128
            if hw:
                S.dma(q, xinh_sem[xb], lambda e, xb=xb, tok0=tok0: e.dma_start(out=xin[:, xb, :], in_=x_d[tok0:tok0 + 128, :]),
                      writes=[("xin", xb)])
            else:
                S.dma("pool", xin_sem[xb], lambda e, xb=xb, tok0=tok0: e.dma_start(out=xin[:, xb, :], in_=x_d[tok0:tok0 + 128, :]),
                      writes=[("xin", xb)])

        def P_chain(i, s):
            xb = s % 2
            jj = next_junk()
            S.op("act", lambda e, jj=jj, xb=xb: e.activation(out=junk[:, jj, :], in_=xin[:, xb, :], func=AF.Square, accum_out=st[:, xb:xb + 1]),
                 reads=[("xin", xb)], writes=[("junk", jj), ("st", xb)])
            S.op("dve", lambda e, xb=xb: e.tensor_scalar(out=st[:, 2 + xb:3 + xb], in0=st[:, xb:xb + 1], scalar1=1.0 / D, scalar2=EPS,
                                                         op0=ALU.mult, op1=ALU.add), reads=[("st", xb)], writes=[("st", 2 + xb)])
            S.op("pool", lambda e, xb=xb: e.tensor_tensor(out=st[:, 2 + xb:3 + xb], in0=st[:, 2 + xb:3 + xb], in1=nh[:, 0:1], op=ALU.pow),
                 reads=[("st", 2 + xb), "nh"], writes=[("st", 2 + xb)])
            S.op("dve", lambda e, xb=xb, s=s: e.scalar_tensor_tensor(out=xs[:, s, :], in0=xin[:, xb, :], scalar=st[:, 2 + xb:3 + xb], in1=Gn[:],
                                                                op0=ALU.mult, op1=ALU.mult),
                 reads=[("xin", xb), ("st", 2 + xb), "Gn"], writes=[("xs", s)])

        def P_back(i, s):
            hb = i % 2
            b = alloc_bank()
            tpv = bank(b).bitcast(BF16).rearrange("p (a c) -> p a c", a=8)
            for dc in range(8):
                S.op("pe", lambda e, dc=dc, s=s, tpv=tpv: e.transpose(out=tpv[:, dc, :], in_=xs[:, s, dc * 128:(dc + 1) * 128], identity=ident[:]),
                     reads=[("xs", s), "ident"], writes=[PS(b)], inc=(dc == 7))
            S.op("act", lambda e, hb=hb, s=s, tpv=tpv: e.activation(out=hT[:, hb, :, s * 128:(s + 1) * 128], in_=tpv, func=AF.Copy),
                 reads=[PS(b)], writes=[("hT", hb, s)])

        def stage_P(i):
            for s in range(NSUB):
                P_front(i, s)
                P_back(i, s)

        def proj_fm(bk, seq, sub, hb):
            wv, key = wpage(seq)
            for dc in range(8):
                S.op("pe", lambda e, bk=bk, wv=wv, dc=dc, sub=sub, hb=hb: e.matmul(
                    bank(bk), lhsT=wv[:, dc, sub * 128:(sub + 1) * 128], rhs=hT[:, hb, dc, :], start=(dc == 0), stop=(dc == 7)),
                    reads=[key] + HT(hb), writes=[PS(bk)], inc=(dc == 7))

        def A_group(i, g, cur):
            hb = i % 2
            slot = i % 2
            pa_seq = cur + (0 if g < 4 else 2)
            ba = alloc_bank()
            bg = alloc_bank()
            proj_fm(ba, pa_seq, g % 4, hb)
            proj_fm(bg, pa_seq + 1, g % 4, hb)
            j = g % 2
            S.op("act", lambda e, bg=bg, j=j: e.activation(out=th[:, j, :], in_=bank(bg), func=AF.Tanh, scale=0.5),
                 reads=[PS(bg)], writes=[("th", j)])
            S.op("dve", lambda e, ba=ba, j=j, slot=slot, g=g: e.scalar_tensor_tensor(
                out=hg[:, slot, g, 16:528], in0=th[:, j, :], scalar=1.0, in1=bank(ba), op0=ALU.add, op1=ALU.mult),
                reads=[("th", j), PS(ba)], writes=[("hg", slot, g)])
            if g == 3:
                release(cur, cur + 1)
            if g == 7:
                release(cur + 2, cur + 3)

        def A_finish(i):
            slot = i % 2
            ws["cur"] += 4
            if i > 0:
                ps_ = (i - 1) % 2
                S.op("pool", lambda e, ps_=ps_, slot=slot: e.tensor_copy(out=hg[:, ps_, :, 528:544], in_=hg[:, slot, :, 16:32]),
                     reads=[("hg", slot, g) for g in range(8)], writes=[("hgr", ps_)])
                S.op("pool", lambda e, ps_=ps_, slot=slot: e.tensor_copy(out=hg[:, slot, :, 0:16], in_=hg[:, ps_, :, 512:528]),
                     reads=[("hg", ps_, g) for g in range(8)], writes=[("hgl", slot)])
            if i == NT - 1:
                S.op("pool", lambda e, slot=slot: e.memset(hg[:, slot, :, 528:544], 0.0), writes=[("hgr", slot)])

        def stage_A(i):
            cur = ws["cur"]
            for g in range(8):
                A_group(i, g, cur)
            A_finish(i)

        def stage_C(i, head_fn=None, v_fn=None, p_hook=None, early_norm=False):
            hb = i % 2
            slot = i % 2
            cur = ws["cur"]
            bm = alloc_bank()
            pinned.add(bm)
            bq = alloc_bank()
            pinned.add(bq)

            def stats(g):
                S.op("pe", lambda e, g=g: e.matmul(bank(bm), lhsT=onesm[:], rhs=cb[:, g, :], start=(g == 0), stop=(g == 7)),
                     reads=[("cb", g), "onesm"], writes=[PS(bm)], inc=False)
                S.op("pe", lambda e, g=g: e.matmul(bank(bq), lhsT=onesm[:], rhs=sq[:, g % 4, :], start=(g == 0), stop=(g == 7)),
                     reads=[("sq", g % 4), "onesm"], writes=[PS(bq)], inc=True)

            for g in range(8):
                seq = cur + DIAG_OFF[g]
                sl = wslot(seq)
                dv = ring[:, sl, 0:KW * 128].rearrange("p (k c) -> p k c", k=KW)
                bc = alloc_bank()
                pe_taps = list(range(NDVE, KW))
                for k in pe_taps:
                    S.op("pe", lambda e, bc=bc, dv=dv, k=k, g=g: e.matmul(
                        bank(bc), lhsT=dv[:, k, :], rhs=hg[:, slot, g, 1 + k:1 + k + 512], start=(k == pe_taps[0]), stop=(k == KW - 1)),
                        reads=[("ring", sl), ("hg", slot, g), ("hgl", slot), ("hgr", slot)], writes=[PS(bc)], inc=(k == KW - 1))
                release(seq)
                ja = g % 2
                for k in range(NDVE):
                    if k == 0:
                        S.op("dve", lambda e, k=k, g=g, ja=ja: e.tensor_scalar(
                            out=t1[:, ja, :], in0=hg[:, slot, g, 1 + k:1 + k + 512], scalar1=colh[:, g, k:k + 1], scalar2=None, op0=ALU.mult),
                            reads=[("hg", slot, g), ("hgl", slot), ("hgr", slot), ("colp", g)], writes=[("t1", ja)])
                    else:
                        S.op("dve", lambda e, k=k, g=g, ja=ja: e.scalar_tensor_tensor(
                            out=t1[:, ja, :], in0=hg[:, slot, g, 1 + k:1 + k + 512], scalar=colh[:, g, k:k + 1], in1=t1[:, ja, :],
                            op0=ALU.mult, op1=ALU.add),
                            reads=[("hg", slot, g), ("hgl", slot), ("hgr", slot), ("t1", ja), ("colp", g)], writes=[("t1", ja)])
                if NDVE > 0:
                    S.op("dve", lambda e, bc=bc, g=g, ja=ja: e.scalar_tensor_tensor(
                        out=cb[:, g, :], in0=bank(bc), scalar=colp[:, g, 31:32], in1=t1[:, ja, :], op0=ALU.add, op1=ALU.add),
                        reads=[PS(bc), ("colp", g), ("t1", ja)], writes=[("cb", g)])
                else:
                    S.op("act", lambda e, bc=bc, g=g: e.activation(out=cb[:, g, :], in_=bank(bc), func=AF.Identity, bias=colp[:, g, 31:32]),
                         reads=[PS(bc), ("colp", g)], writes=[("cb", g)])
                S.op("pool", lambda e, g=g: e.tensor_tensor(out=sq[:, g % 4, :], in0=cb[:, g, :], in1=cb[:, g, :], op=ALU.mult),
                     reads=[("cb", g)], writes=[("sq", g % 4)])
                if g >= STATS_LAG:
                    stats(g - STATS_LAG)
                if head_fn is not None:
                    head_fn(g)
                if deferred and g < 2:
                    deferred.pop(0)()
                if p_hook is not None:
                    p_hook(g)
            for g_ in range(8 - STATS_LAG, 8):
                stats(g_)
            def Gp(g):
                bgc = alloc_bank()
                proj_fm(bgc, cur + 12 + g // 4, g % 4, hb)
                if g == 3:
                    release(cur + 12)
                if g == 7:
                    release(cur + 13)
                S.op("act", lambda e, bgc=bgc, g=g: e.activation(out=sgc[:, g, :], in_=bank(bgc), func=AF.Silu),
                     reads=[PS(bgc)], writes=[("sgc", g)])

            def postproc():
                S.op("act", lambda e: e.activation(out=mean_sb[:], in_=bank(bm), func=AF.Copy), reads=[PS(bm)], writes=["mean_sb"])
                S.op("dve", lambda e: e.scalar_tensor_tensor(out=nmr_sb[:], in0=mean_sb[:], scalar=-1.0, in1=mean_sb[:], op0=ALU.mult, op1=ALU.mult),
                     reads=["mean_sb"], writes=["nmr_sb"])
                S.op("dve", lambda e: e.tensor_tensor(out=rstd_sb[:], in0=bank(bq), in1=nmr_sb[:], op=ALU.add),
                     reads=[PS(bq), "nmr_sb"], writes=["rstd_sb"])
                pinned.discard(bm)
                pinned.discard(bq)
                S.op("dve", lambda e: e.tensor_scalar(out=rstd_sb[:], in0=rstd_sb[:], scalar1=0.0, scalar2=EPS, op0=ALU.max, op1=ALU.add),
                     reads=["rstd_sb"], writes=["rstd_sb"])
                S.op("act", lambda e: e.activation(out=rstd_sb[:], in_=rstd_sb[:], func=AF.Sqrt), reads=["rstd_sb"], writes=["rstd_sb"])
                S.op("dve", lambda e: e.reciprocal(out=rstd_sb[:], in_=rstd_sb[:]), reads=["rstd_sb"], writes=["rstd_sb"])
                S.op("dve", lambda e: e.scalar_tensor_tensor(out=nmr_sb[:], in0=mean_sb[:], scalar=-1.0, in1=rstd_sb[:], op0=ALU.mult, op1=ALU.mult),
                     reads=["mean_sb", "rstd_sb"], writes=["nmr_sb"])

            if early_norm:
                postproc()

                def Gn_(g):
                    Gp(g)
                    norm_group(g)
                v_fn(Gn_)
            else:
                v_fn(Gp)
                postproc()
            ws["cur"] += 16

        def norm_a(g):
            j = g % 2
            S.op("dve", lambda e, g=g, j=j: e.tensor_tensor(out=t1[:, j, :], in0=cb[:, g, :], in1=rstd_sb[:], op=ALU.mult),
                 reads=[("cb", g), "rstd_sb"], writes=[("t1", j)])
            S.op("dve", lambda e, j=j: e.tensor_tensor(out=t1[:, j, :], in0=t1[:, j, :], in1=nmr_sb[:], op=ALU.add),
                 reads=[("t1", j), "nmr_sb"], writes=[("t1", j)])
            S.op("act", lambda e, g=g, j=j: e.activation(out=sw[:, j, :], in_=t1[:, j, :], func=AF.Silu,
                                                         scale=colp[:, g, 32:33], bias=colp[:, g, 33:34]),
                 reads=[("t1", j), ("colp", g)], writes=[("sw", j)])

        def norm_b(g):
            j = g % 2
            S.op("dve", lambda e, g=g, j=j: e.tensor_tensor(out=ycv[:, g, :], in0=sw[:, j, :], in1=sgc[:, g, :], op=ALU.mult),
                 reads=[("sw", j), ("sgc", g)], writes=[("ycv", g)])

        def norm_group(g):
            norm_a(g)
            norm_b(g)

        def S_head(i, h, cur):
            hb = i % 2
            pu_seq = cur + (0 if h < 4 else 7)
            bu = alloc_bank()
            bgs = alloc_bank()
            proj_fm(bu, pu_seq, h % 4, hb)
            proj_fm(bgs, pu_seq + 1, h % 4, hb)
            if h == 3:
                release(cur + 0, cur + 1)
            if h == 7:
                release(cur + 7, cur + 8)
            j = h % 2
            YK = [("ysg", h, s) for s in range(NSUB)]
            S.op("act", lambda e, bgs=bgs, j=j: e.activation(out=sgs[:, j, :], in_=bank(bgs), func=AF.Silu),
                 reads=[PS(bgs)], writes=[("sgs", j)])
            S.op("dve", lambda e, bu=bu, j=j, h=h: e.tensor_tensor(out=ysg[:, h, :], in0=bank(bu), in1=sgs[:, j, :], op=ALU.mult),
                 reads=[PS(bu), ("sgs", j)], writes=YK)

        def stage_V(i, Gp, Pb):
            hb = i % 2
            cur = ws["cur"]
            wv0, k0 = wpage(cur + 14)
            wv1, k1 = wpage(cur + 15)

            def pv(s):
                bv = alloc_pair()
                for dc in range(8):
                    for half in range(2):
                        wv, key = (wv0, k0) if half == 0 else (wv1, k1)
                        S.op("pe", lambda e, bv=bv, dc=dc, half=half, wv=wv, s=s: e.matmul(
                            bank(bv + half), lhsT=hT[:, hb, dc, s * 128:(s + 1) * 128], rhs=wv[:, dc, :], start=(dc == 0), stop=(dc == 7)),
                            reads=[("hT", hb, s), key], writes=[PS(bv + half)], inc=(dc == 7 and half == 1))
                vj = s % 2
                pvv = pp[bv // 2]
                PV = [PS(bv), PS(bv + 1)]
                jj = next_junk()
                S.op("act", lambda e, jj=jj, pvv=pvv, vj=vj: e.activation(out=junk[:, jj, :], in_=pvv[:, :], func=AF.Identity, accum_out=st[:, 4 + vj:5 + vj]),
                     reads=PV, writes=[("junk", jj), ("st", 4 + vj)])
                jj = next_junk()
                S.op("act", lambda e, jj=jj, pvv=pvv, vj=vj: e.activation(out=junk[:, jj, :], in_=pvv[:, :], func=AF.Square, accum_out=st[:, 6 + vj:7 + vj]),
                     reads=PV, writes=[("junk", jj), ("st", 6 + vj)])
                S.op("dve", lambda e, vj=vj: e.tensor_scalar(out=st[:, 8 + vj:9 + vj], in0=st[:, 4 + vj:5 + vj], scalar1=1.0 / D, scalar2=None, op0=ALU.mult),
                     reads=[("st", 4 + vj)], writes=[("st", 8 + vj)])
                S.op("dve", lambda e, pvv=pvv, vj=vj: e.scalar_tensor_tensor(out=vg[:, vj, :], in0=pvv[:, :], scalar=st[:, 8 + vj:9 + vj], in1=Gv[:],
                                                                             op0=ALU.subtract, op1=ALU.mult),
                     reads=PV + [("st", 8 + vj), "Gv"], writes=[("vg", vj)])
                S.op("dve", lambda e, vj=vj: e.scalar_tensor_tensor(out=st[:, 10 + vj:11 + vj], in0=st[:, 8 + vj:9 + vj], scalar=-1.0, in1=st[:, 8 + vj:9 + vj],
                                                                    op0=ALU.mult, op1=ALU.mult),
                     reads=[("st", 8 + vj)], writes=[("st", 10 + vj)])
                S.op("dve", lambda e, vj=vj: e.scalar_tensor_tensor(out=st[:, 12 + vj:13 + vj], in0=st[:, 6 + vj:7 + vj], scalar=1.0 / D, in1=st[:, 10 + vj:11 + vj],
                                                                    op0=ALU.mult, op1=ALU.add),
                     reads=[("st", 6 + vj), ("st", 10 + vj)], writes=[("st", 12 + vj)])
                S.op("dve", lambda e, vj=vj: e.tensor_scalar(out=st[:, 12 + vj:13 + vj], in0=st[:, 12 + vj:13 + vj], scalar1=0.0, scalar2=EPS,
                                                             op0=ALU.max, op1=ALU.add),
                     reads=[("st", 12 + vj)], writes=[("st", 12 + vj)])
                S.op("pool", lambda e, vj=vj: e.tensor_tensor(out=st[:, 12 + vj:13 + vj], in0=st[:, 12 + vj:13 + vj], in1=nh[:, 0:1], op=ALU.pow),
                     reads=[("st", 12 + vj), "nh"], writes=[("st", 12 + vj)])
                S.op("dve", lambda e, vj=vj: e.tensor_scalar(out=vg[:, vj, :], in0=vg[:, vj, :], scalar1=st[:, 12 + vj:13 + vj], scalar2=None, op0=ALU.mult),
                     reads=[("vg", vj), ("st", 12 + vj)], writes=[("vg", vj)])

            def mix(s):
                vj = s % 2
                bm2 = alloc_pair()
                pm = pp[bm2 // 2]
                for h in range(8):
                    S.op("pe", lambda e, pm=pm, h=h, vj=vj: e.matmul(
                        pm[:, h * 128:(h + 1) * 128], lhsT=vg[:, vj, h * 128:(h + 1) * 128], rhs=wst[:, h * 128:(h + 1) * 128],
                        start=True, stop=False, skip_group_check=True),
                        reads=[("vg", vj), "wst"], writes=[PS(bm2), PS(bm2 + 1)], inc=False)
                    S.op("pe", lambda e, pm=pm, h=h: e.matmul(
                        pm[:, h * 128:(h + 1) * 128], lhsT=lsm[:, h, :], rhs=rsm[:, h, :], start=False, stop=True, skip_group_check=True),
                        reads=["lsm"] + RSM, writes=[PS(bm2), PS(bm2 + 1)], inc=(h == 7))
                S.op("dve", lambda e, pm=pm, s=s: e.tensor_tensor(
                    out=ysg[:, :, s * 128:(s + 1) * 128], in0=pm[:, :].rearrange("p (h c) -> p h c", h=8),
                    in1=ysg[:, :, s * 128:(s + 1) * 128], op=ALU.mult),
                    reads=[PS(bm2), PS(bm2 + 1)] + [("ysg", h, s) for h in range(8)], writes=[("ysg", h, s) for h in range(8)])

            Gp(0)
            Gp(1)
            Gp(2)
            Gp(3)
            pv(0)
            Gp(4)
            Gp(5)
            pv(1)
            mix(0)
            Gp(6)
            Gp(7)
            pv(2)
            mix(1)
            pv(3)
            release(cur + 14, cur + 15)
            Pb(0)
            Pb(1)
            mix(2)
            Pb(2)
            Pb(3)
            mix(3)

        ostate = {}

        def O_mm(i, s, ecs, first, last):
            bo = ostate[s]
            for ec in ecs:
                for half in range(2):
                    if ec < 8:
                        lt = ycv[:, ec, s * 128:(s + 1) * 128]
                        rk = ("ycv", ec)
                    else:
                        lt = ysg[:, ec - 8, s * 128:(s + 1) * 128]
                        rk = ("ysg", ec - 8, s)
                    S.op("pe", lambda e, bo=bo, half=half, lt=lt, ec=ec: e.matmul(
                        bank(bo + half), lhsT=lt, rhs=wout[:, ec, half * 512:(half + 1) * 512],
                        start=(first and ec == ecs[0]), stop=(last and ec == ecs[-1])),
                        reads=[rk, ("wout", ec // 4)], writes=[PS(bo + half)], inc=(last and ec == ecs[-1] and half == 1))

        def O_load(i, s):
            r = (i * NSUB + s) % 3
            tok0 = i * T + s * 128
            S.dma("pool", xres_sem[r], lambda e, r=r, tok0=tok0: e.dma_start(out=xres[:, r, :], in_=x_d[tok0:tok0 + 128, :]),
                  writes=[("xres", r)])

        def O_a1(i, s, load=True):
            if load:
                O_load(i, s)
            ostate[s] = alloc_pair()
            O_mm(i, s, list(range(8, 16)), True, False)

        def O_a2(i, s):
            r = (i * NSUB + s) % 3
            bo = ostate[s]
            O_mm(i, s, list(range(8)), False, True)
            po = pp[bo // 2]
            S.op("dve", lambda e, po=po, r=r: e.tensor_tensor(out=xres[:, r, :], in0=po[:, :], in1=xres[:, r, :], op=ALU.add),
                 reads=[PS(bo), PS(bo + 1), ("xres", r)], writes=[("xres", r)])
            jj = next_junk()
            S.op("act", lambda e, jj=jj, r=r: e.activation(out=junk[:, jj, :], in_=xres[:, r, :], func=AF.Square, accum_out=st[:, 16 + r:17 + r]),
                 reads=[("xres", r)], writes=[("junk", jj), ("st", 16 + r)])

        def O_a(i, s):
            O_a1(i, s)
            O_a2(i, s)

        def O_b(i, s):
            r = (i * NSUB + s) % 3
            tok0 = i * T + s * 128
            S.op("dve", lambda e, r=r: e.tensor_scalar(out=st[:, 20 + r:21 + r], in0=st[:, 16 + r:17 + r], scalar1=1.0 / D, scalar2=EPS,
                                                       op0=ALU.mult, op1=ALU.add), reads=[("st", 16 + r)], writes=[("st", 20 + r)])
            S.op("pool", lambda e, r=r: e.tensor_tensor(out=st[:, 20 + r:21 + r], in0=st[:, 20 + r:21 + r], in1=nh[:, 0:1], op=ALU.pow),
                 reads=[("st", 20 + r), "nh"], writes=[("st", 20 + r)])
            S.op("dve", lambda e, r=r: e.scalar_tensor_tensor(out=xres[:, r, :], in0=xres[:, r, :], scalar=st[:, 20 + r:21 + r], in1=Gfin[:],
                                                              op0=ALU.mult, op1=ALU.mult),
                 reads=[("xres", r), ("st", 20 + r), "Gfin"], writes=[("xres", r)])
            S.dma("sp", out_sem[r], lambda e, r=r, tok0=tok0: e.dma_start(out=out_d[tok0:tok0 + 128, :], in_=xres[:, r, :]),
                  reads=[("xres", r)])

        deferred = []

        def stage_O_all(i):
            O_a1(i, 0)
            O_a1(i, 1)
            O_a2(i, 0)
            O_a2(i, 1)
            O_b(i, 0)
            O_a(i, 2)
            O_b(i, 1)
            O_a(i, 3)
            deferred.append(lambda i=i: O_b(i, 2))
            deferred.append(lambda i=i: O_b(i, 3))

        P_load(0, 0, hw=True)
        P_load(0, 1, hw=True)
        gn_sem = S.dma_sem("gn")
        S.dma("sp", gn_sem, lambda e: e.dma_start(out=Gn[:], in_=norm_g_d[0:1, :].to_broadcast([128, 1024])), writes=["Gn"])
        P_chain(0, 0)
        P_load(0, 2, hw=True)
        P_chain(0, 1)
        P_load(0, 3, hw=True)
        P_back(0, 0)
        P_back(0, 1)
        prologue_params()
        prefetch()
        P_chain(0, 2)
        P_chain(0, 3)
        P_back(0, 2)
        P_back(0, 3)
        if NT > 1:
            for s_ in range(NSUB):
                P_front(1, s_, hw=True, q="act")
        cur0 = ws["cur"]
        for g in range(8):
            A_group(0, g, cur0)
            if NT > 1 and g % 2 == 1:
                P_back(1, g // 2)
        A_finish(0)
        wsems = [S.dma_sem("wo%d" % i) for i in range(4)]
        for q in range(4):
            S.dma("pool", wsems[q], lambda e, q=q: e.dma_start(
                out=wout[:, q * 4:(q + 1) * 4, :],
                in_=w_out_d[q * 512:(q + 1) * 512, :].rearrange("(c p) d -> p c d", p=128)), writes=[("wout", q)])
        WOUT = [("wout", q) for q in range(4)]

        if NT > 1:
            stage_A(1)
        for i in range(NT):
            nxt = i + 2 < NT
            cur = ws["cur"]

            def vfn(Gp, i=i, nxt=nxt):
                stage_V(i, Gp, (lambda s_: P_back(i + 2, s_)) if nxt else (lambda s_: None))

            def phook(g, i=i, nxt=nxt):
                if not nxt or g % 2 == 0:
                    return
                s_ = g // 2
                P_chain(i + 2, s_)
                if s_ + 2 < NSUB:
                    P_load(i + 2, s_ + 2)

            if nxt:
                P_load(i + 2, 0)
                P_load(i + 2, 1)
            stage_C(i, head_fn=lambda h, i=i, cur=cur: S_head(i, h, cur), v_fn=vfn, p_hook=phook, early_norm=False)
            if nxt:
                cura = ws["cur"]
                A_group(i + 2, 0, cura)
                for g in range(8):
                    if g + 1 < 8:
                        A_group(i + 2, g + 1, cura)
                    norm_a(g)
                    if g > 0:
                        norm_b(g - 1)
                norm_b(7)
                A_finish(i + 2)
            if nxt:
                stage_O_all(i)
            else:
                for s_ in range(NSUB):
                    O_a1(i, s_, load=False)
                    norm_a(2 * s_)
                    if s_ > 0:
                        norm_b(2 * s_ - 1)
                    norm_a(2 * s_ + 1)
                    norm_b(2 * s_)
                norm_b(7)
                O_load(i, 0)
                O_load(i, 1)
                O_a2(i, 0)
                O_load(i, 2)
                O_a2(i, 1)
                O_b(i, 0)
                O_a2(i, 2)
                O_b(i, 1)
                O_load(i, 3)
                O_a2(i, 3)
                deferred.append(lambda i=i: O_b(i, 2))
                deferred.append(lambda i=i: O_b(i, 3))
        while deferred:
            deferred.pop(0)()
        print("sbuf bytes remaining:", nc.sbuf_bytes_remaining)
        S.finish(["sp"], out_sem)
        S.emit()
    return nc


_PROG_CACHE = {}


def _run(x, params, n_cores):
    S_TOK = x.shape[1]
    if S_TOK not in _PROG_CACHE:
        _PROG_CACHE[S_TOK] = build_program(S_TOK)
    nc = _PROG_CACHE[S_TOK]
    in_maps = []
    for c in range(n_cores):
        m = dict(params)
        m["x"] = np.ascontiguousarray(x[c])
        in_maps.append(m)
    res = run_bass_kernel_spmd(nc, in_maps, core_ids=list(range(n_cores)))
    return np.stack([r["out"] for r in res.results], axis=0)


def _params(norm_g, w_in, conv_w, conv_b, conv_ln_g, conv_ln_b, sgu_ln_g, sgu_ln_b, w_s, b_s, w_out, final_g):
    f = lambda a: np.ascontiguousarray(np.asarray(a, dtype=np.float32))
    return {
        "w_in": f(w_in[0]), "w_out": f(w_out[0]), "conv_w": f(conv_w[0]),
        "conv_b": f(conv_b[0]).reshape(1, D), "conv_ln_g": f(conv_ln_g[0]).reshape(1, D),
        "conv_ln_b": f(conv_ln_b[0]).reshape(1, D), "sgu_ln_g": f(sgu_ln_g[0]).reshape(1, D),
        "sgu_ln_b": f(sgu_ln_b[0]).reshape(1, D), "norm_g": f(norm_g[0]).reshape(1, D),
        "final_g": f(final_g).reshape(1, D), "w_s": f(w_s[0]), "b_s": f(b_s[0]).reshape(1, D),
    }


def kernel(x, norm_g, w_in, conv_w, conv_b, conv_ln_g, conv_ln_b, sgu_ln_g, sgu_ln_b, w_s, b_s, w_out, final_g):
    x = np.asarray(x, dtype=np.float32)
    params = _params(norm_g, w_in, conv_w, conv_b, conv_ln_g, conv_ln_b, sgu_ln_g, sgu_ln_b, w_s, b_s, w_out, final_g)
    out = _run(x, params, x.shape[0])
    return out.astype(np.float32)
```

```python
import numpy as np
import concourse.bass as bass
import concourse.mybir as mybir
from concourse.bass_utils import run_bass_kernel_spmd
from contextlib import ExitStack

F32 = mybir.dt.float32
BF16 = mybir.dt.bfloat16
AF = mybir.ActivationFunctionType
ALU = mybir.AluOpType


class Sched:
    ENG = ("pe", "act", "dve", "pool", "sp")

    def __init__(self, nc, es):
        self.nc = nc
        self.es = es
        self.streams = {e: [] for e in self.ENG}
        self.sems = {}
        self.count = {}
        for e in ("pe", "act", "dve", "pool"):
            self.sems[e] = es.enter_context(nc.semaphore("prog_" + e))
            self.count[e] = 0
        self.seen = {e: {} for e in self.ENG}
        self.bufs = {}
        self.pending = {e: False for e in self.ENG}

    def dma_sem(self, name):
        s = self.es.enter_context(self.nc.semaphore("dma_" + name))
        self.sems[name] = s
        self.count[name] = 0
        return name

    def _deps(self, eng, reads, writes):
        deps = {}

        def add(clk):
            if clk is None:
                return
            k, v = clk
            if k == "pe" and eng == "pe":
                return
            if v > self.count[k]:
                raise RuntimeError(f"dependency on un-milestoned op {k}:{v} > {self.count[k]}")
            if deps.get(k, 0) < v:
                deps[k] = v

        for b in reads:
            st = self.bufs.get(b)
            if st is not None:
                add(st["w"])
        for b in writes:
            st = self.bufs.get(b)
            if st is not None:
                add(st["w"])
                for r in st["r"]:
                    add(r)
        out = []
        for k, v in deps.items():
            if self.seen[eng].get(k, 0) < v:
                self.seen[eng][k] = v
                out.append((k, v))
        return out

    def _mark(self, clk, reads, writes):
        for b in reads:
            st = self.bufs.setdefault(b, {"w": None, "r": []})
            st["r"].append(clk)
        for b in writes:
            self.bufs[b] = {"w": clk, "r": []}

    def op(self, eng, fn, reads=(), writes=(), inc=True):
        waits = self._deps(eng, reads, writes)
        clk = (eng, self.count[eng] + 1)
        if inc:
            self.count[eng] += 1
            self.pending[eng] = False
        else:
            self.pending[eng] = True
        self._mark(clk, reads, writes)
        self.streams[eng].append((waits, fn, (eng, 1) if inc else None))

    def dma(self, queue, sem, fn, reads=(), writes=()):
        waits = self._deps(queue, reads, writes)
        self.count[sem] += 16
        clk = (sem, self.count[sem])
        self._mark(clk, reads, writes)
        self.streams[queue].append((waits, fn, (sem, 16)))

    def finish(self, queues, sems):
        for q in queues:
            waits = []
            for s in sems:
                waits.append((s, self.count[s]))
            self.streams[q].append((waits, None, None))

    def emit(self):
        nc = self.nc
        for e in ("pe",):
            assert not self.pending[e], "trailing un-milestoned PE ops"
        sems = self.sems
        streams = self.streams

        def replay(name):
            def run(eng):
                for waits, fn, inc in streams[name]:
                    for k, v in waits:
                        eng.wait_ge(sems[k], v)
                    if fn is None:
                        continue
                    ins = fn(eng)
                    if inc is not None:
                        ins.then_inc(sems[inc[0]], inc[1])
            return run

        with nc.Block() as block:
            block.tensor(replay("pe"))
            block.scalar(replay("act"))
            block.vector(replay("dve"))
            block.gpsimd(replay("pool"))
            block.sync(replay("sp"))


D = 1024
DIN = 6144
T = 512
NSUB = 4
KW = 31
NSLOT = 4
NDVE = 6
STATS_LAG = 3
EPS = 1e-6


def build_program(S_TOK):
    NT = S_TOK // T
    nc = bass.Bass("TRN2", target_bir_lowering=False)

    def din(name, shape):
        return nc.dram_tensor(name, shape, F32, kind="ExternalInput").ap()

    x_d = din("x", [S_TOK, D])
    w_in_d = din("w_in", [D, DIN])
    w_out_d = din("w_out", [2 * D, D])
    conv_w_d = din("conv_w", [KW, D])
    conv_b_d = din("conv_b", [1, D])
    cln_g_d = din("conv_ln_g", [1, D])
    cln_b_d = din("conv_ln_b", [1, D])
    sln_g_d = din("sgu_ln_g", [1, D])
    sln_b_d = din("sgu_ln_b", [1, D])
    norm_g_d = din("norm_g", [1, D])
    final_g_d = din("final_g", [1, D])
    w_s_d = din("w_s", [8, 128, 128])
    b_s_d = din("b_s", [1, D])
    out_d = nc.dram_tensor("out", [S_TOK, D], F32, kind="ExternalOutput").ap()
    win_s = nc.dram_tensor("win_s", [12, 128, 4096], BF16, kind="Internal").ap()
    diag_s = nc.dram_tensor("diag_s", [8, 128, KW * 128], BF16, kind="Internal").ap()

    with ExitStack() as es:
        def sb(name, shape, dt):
            return es.enter_context(nc.sbuf_tensor(name, shape, dt))

        ring = sb("ring", [128, NSLOT, 4096], BF16)
        wout = sb("wout", [128, 16, 1024], BF16)
        hT = sb("hT", [128, 2, 8, 512], BF16)
        hg = sb("hg", [128, 2, 8, 544], BF16)
        sgc = sb("sgc", [128, 8, 512], BF16)
        cb = sb("cb", [128, 8, 512], BF16)
        ycv = sb("ycv", [128, 8, 512], BF16)
        ysg = sb("ysg", [128, 8, 512], BF16)
        xin = sb("xin", [128, 2, 1024], F32)
        xs = sb("xs", [128, 4, 1024], BF16)
        junk = sb("junk", [128, 2, 1024], BF16)
        jstate = {"n": 0}

        def next_junk():
            j = jstate["n"] % 2
            jstate["n"] += 1
            return j
        th = sb("th", [128, 2, 512], BF16)
        sgs = sb("sgs", [128, 2, 512], BF16)
        sq = sb("sq", [128, 4, 512], BF16)
        mean_sb = sb("mean_sb", [128, 512], F32)
        rstd_sb = sb("rstd_sb", [128, 512], F32)
        nmr_sb = sb("nmr_sb", [128, 512], F32)
        t1 = sb("t1", [128, 2, 512], F32)
        sw = sb("sw", [128, 2, 512], BF16)
        vg = sb("vg", [128, 2, 1024], BF16)
        wst = sb("wst", [128, 1024], BF16)
        xres = sb("xres", [128, 3, 1024], F32)
        Gfin = sb("Gfin", [128, 1024], F32)
        Gv = sb("Gv", [128, 1024], F32)
        Gn = sb("Gn", [128, 1024], F32)
        colp = sb("colp", [128, 8, 35], F32)
        colh = sb("colh", [128, 8, 31], F32)
        ident = sb("ident", [128, 128], BF16)
        identf = sb("identf", [128, 128], F32)
        onesm = sb("onesm", [128, 128], BF16)
        onec = sb("onec", [128, 1], BF16)
        lsm = sb("lsm", [128, 8, 128], BF16)
        rsm = sb("rsm", [128, 8, 128], BF16)
        nh = sb("nh", [128, 1], F32)
        st = sb("st", [128, 32], F32)

        pp = [es.enter_context(nc.psum_tensor("pp%d" % j, [128, 1024], F32)) for j in range(4)]

        def bank(b):
            return pp[b // 2][:, (b % 2) * 512:(b % 2) * 512 + 512]

        def PS(b):
            return ("ps", b)

        S = Sched(nc, es)
        ring_sem = [S.dma_sem("ring%d" % i) for i in range(NSLOT)]
        ringc_sem = [S.dma_sem("ringc%d" % i) for i in range(NSLOT)]
        xin_sem = [S.dma_sem("xin%d" % i) for i in range(2)]
        xinh_sem = [S.dma_sem("xinh%d" % i) for i in range(2)]
        xres_sem = [S.dma_sem("xres%d" % i) for i in range(3)]
        out_sem = [S.dma_sem("out%d" % i) for i in range(3)]
        pro_sem = S.dma_sem("pro")
        pro2_sem = S.dma_sem("pro2")

        bstate = {"next": 0}

        def alloc_bank():
            b = bstate["next"]
            bstate["next"] = (b + 1) % 8
            return b

        def alloc_pair():
            b = bstate["next"]
            if b % 2:
                b = (b + 1) % 8
            bstate["next"] = (b + 2) % 8
            return b

        def mk_ident(t, key):
            S.op("pool", lambda e: e.memset(t[:], 0.0), writes=[key])
            S.op("pool", lambda e: e.affine_select(out=t[:], in_=t[:], compare_op=ALU.not_equal, fill=1.0,
                                                    base=0, pattern=[[-1, 128]], channel_multiplier=1),
                 reads=[key], writes=[key])

        mk_ident(ident, "ident")
        mk_ident(identf, "identf")
        S.op("pool", lambda e: e.memset(onesm[:], 1.0 / 1024), writes=["onesm"])
        S.op("pool", lambda e: e.memset(onec[:], 1.0), writes=["onec"])
        S.op("pool", lambda e: e.memset(nh[:], -0.5), writes=["nh"])
        S.op("pool", lambda e: e.memset(lsm[:], 0.0), writes=["lsm"])
        S.op("pool", lambda e: e.memset(lsm[32:33, :, :], 1.0), reads=["lsm"], writes=["lsm"])
        S.op("pool", lambda e: e.memset(lsm[64:65, :, :], 1.0), reads=["lsm"], writes=["lsm"])
        S.op("pool", lambda e: e.memset(rsm[:], 0.0), writes=["rsm0", "rsm1", "rsm2"])
        S.op("pool", lambda e: e.memset(hg[:, 0, :, 0:16], 0.0), writes=[("hgl", 0)])

        RSM = ["rsm0", "rsm1", "rsm2"]

        def prologue_params():
            def pdma(queue, fn, reads=(), writes=(), sem=None):
                S.dma(queue, sem or pro_sem, fn, reads=reads, writes=writes)

            bsems = [S.dma_sem("b%d" % i) for i in range(6)]
            S.dma("sp", bsems[0], lambda e: e.dma_start(out=Gfin[:], in_=final_g_d[0:1, :].to_broadcast([128, 1024])), writes=["Gfin"])
            S.dma("sp", bsems[1], lambda e: e.dma_start(out=Gv[:], in_=sln_g_d[0:1, :].to_broadcast([128, 1024])), writes=["Gv"])

            R = xres[:, 0, :]
            rsems = [S.dma_sem("r%d" % i) for i in range(4)]
            S.dma("sp", rsems[0], lambda e: e.dma_start(out=R[0:31, :], in_=conv_w_d[:, :]), writes=[("xres", 0)])
            S.dma("sp", rsems[1], lambda e: e.dma_start(out=R[31:32, :], in_=conv_b_d[:, :]), writes=[("xres", 0)])
            S.dma("sp", rsems[2], lambda e: e.dma_start(out=R[32:33, :], in_=cln_g_d[:, :]), writes=[("xres", 0)])
            S.dma("sp", rsems[3], lambda e: e.dma_start(out=R[33:34, :], in_=cln_b_d[:, :]), writes=[("xres", 0)])
            for c in range(8):
                b = alloc_bank()
                S.op("pe", lambda e, b=b, c=c: e.matmul(bank(b)[:, 0:34], lhsT=R[0:34, c * 128:(c + 1) * 128],
                                                        rhs=identf[0:34, 0:34], start=True, stop=True),
                     reads=[("xres", 0), "identf"], writes=[PS(b)])
                S.op("dve", lambda e, b=b, c=c: e.tensor_copy(out=colp[:, c, 0:34], in_=bank(b)[:, 0:34]),
                     reads=[PS(b)], writes=[("colp", c)])
                S.op("dve", lambda e, c=c: e.tensor_scalar(out=colh[:, c, :], in0=colp[:, c, 0:31], scalar1=0.5, scalar2=None, op0=ALU.mult),
                     reads=[("colp", c)], writes=[("colp", c)])
            COLP = [("colp", c) for c in range(8)]

            wsp = vg
            S.dma("pool", pro_sem, lambda e: e.dma_start(
                out=vg[:, 0, :].rearrange("p (h q) -> p h q", h=8), in_=w_s_d.rearrange("h p q -> p h q")), writes=[("vg", 0)])
            b = alloc_bank()
            tpv = bank(b).bitcast(BF16).rearrange("p (a c) -> p a c", a=8)
            for h in range(8):
                S.op("pe", lambda e, h=h: e.transpose(out=tpv[:, h, :], in_=vg[:, 0, h * 128:(h + 1) * 128], identity=ident[:]),
                     reads=[("vg", 0), "ident"], writes=[PS(b)], inc=(h == 7))
            S.op("dve", lambda e: e.tensor_copy(out=wst[:].rearrange("p (a c) -> p a c", a=8), in_=tpv), reads=[PS(b)], writes=["wst"])
            b2 = alloc_pair()
            for half in range(2):
                S.op("pe", lambda e, half=half: e.matmul(bank(b2 + half)[0:1, :], lhsT=onec[:, 0:1],
                                                         rhs=wst[:, half * 512:(half + 1) * 512], start=True, stop=True),
                     reads=["wst", "onec"], writes=[PS(b2 + half)])
            S.op("dve", lambda e: e.tensor_copy(out=rsm[0:1, :, :].rearrange("o h p -> o (h p)"), in_=pp[b2 // 2][0:1, :]),
                 reads=[PS(b2), PS(b2 + 1)], writes=["rsm0"])
            r0 = xres[0:1, 1, :]
            r1 = t1[0:1, :, :].rearrange("o a n -> o (a n)")
            rb0 = sgs[0:1, :, :].rearrange("o a n -> o (a n)")
            rb1 = sw[0:1, :, :].rearrange("o a n -> o (a n)")
            S.dma("sp", pro2_sem, lambda e: e.dma_start(out=r0, in_=b_s_d[:, :]), writes=[("xres", 1)])
            S.op("dve", lambda e: e.tensor_copy(out=rb0, in_=r0), reads=[("xres", 1)], writes=[("sgs", 0), ("sgs", 1)])
            S.op("dve", lambda e: e.tensor_copy(out=r1, in_=rb0), reads=[("sgs", 0), ("sgs", 1)], writes=[("t1", 0), ("t1", 1)])
            S.op("dve", lambda e: e.tensor_tensor(out=rb1, in0=r0, in1=r1, op=ALU.subtract),
                 reads=[("xres", 1), ("t1", 0), ("t1", 1)], writes=[("sw", 0), ("sw", 1)])
            S.dma("sp", bsems[3], lambda e: e.dma_start(out=rsm[32:33, :, :].rearrange("o h p -> o (h p)"), in_=rb0),
                  reads=[("sgs", 0), ("sgs", 1)], writes=["rsm1"])
            S.dma("sp", bsems[4], lambda e: e.dma_start(out=rsm[64:65, :, :].rearrange("o h p -> o (h p)"), in_=rb1),
                  reads=[("sw", 0), ("sw", 1)], writes=["rsm2"])
            S.dma("pool", bsems[5], lambda e: e.dma_start(out=lsm[0:1, :, :].rearrange("o h p -> o (h p)"), in_=sln_b_d[:, :]),
                  reads=["lsm"], writes=["lsm"])

        pinned = set()

        def alloc_bank():
            while True:
                b = bstate["next"]
                bstate["next"] = (b + 1) % 8
                if b not in pinned:
                    return b

        def alloc_pair():
            while True:
                b = bstate["next"]
                if b % 2:
                    b = (b + 1) % 8
                bstate["next"] = (b + 2) % 8
                if b not in pinned and (b + 1) not in pinned:
                    return b

        stream = []
        A_PAGES = [("w", 0), ("w", 2), ("w", 1), ("w", 3)]
        CS_PAGES = [("w", 6), ("w", 10), ("d", 0), ("d", 1), ("d", 2), ("d", 3), ("d", 4), ("w", 7), ("w", 11),
                    ("d", 5), ("d", 6), ("d", 7), ("w", 4), ("w", 5), ("w", 8), ("w", 9)]
        DIAG_OFF = {0: 2, 1: 3, 2: 4, 3: 5, 4: 6, 5: 9, 6: 10, 7: 11}
        stream += A_PAGES
        if NT > 1:
            stream += A_PAGES
        for i in range(NT):
            stream += CS_PAGES
            if i + 2 < NT:
                stream += A_PAGES
        ws = {"issued": 0, "free": list(range(NSLOT)), "slot_of": {}, "cur": 0}

        done_w = set()
        done_d = set()

        def prefetch():
            while ws["issued"] < len(stream) and ws["free"]:
                q = ws["issued"]
                kind, idx = stream[q]
                sl = ws["free"].pop(0)
                ws["slot_of"][q] = sl
                if kind == "w":
                    if idx not in done_w:
                        done_w.add(idx)
                        S.dma("pool", ringc_sem[sl], lambda e, idx=idx, sl=sl: e.dma_start(
                            out=ring[:, sl, :].rearrange("p (c n) -> p c n", c=8),
                            in_=w_in_d[:, idx * 512:(idx + 1) * 512].rearrange("(c p) n -> p c n", p=128)), writes=[("ring", sl)])
                        S.dma("sp", ring_sem[sl], lambda e, idx=idx, sl=sl: e.dma_start(out=win_s[idx], in_=ring[:, sl, :]),
                              reads=[("ring", sl)], writes=[("win_s", idx)])
                    else:
                        S.dma("sp", ring_sem[sl], lambda e, idx=idx, sl=sl: e.dma_start(out=ring[:, sl, :], in_=win_s[idx]),
                              reads=[("win_s", idx)], writes=[("ring", sl)])
                else:
                    if idx not in done_d:
                        done_d.add(idx)
                        S.op("dve", lambda e, idx=idx, sl=sl: e.scalar_tensor_tensor(
                            out=ring[:, sl, 0:KW * 128].rearrange("p (k c) -> p k c", k=KW),
                            in0=ident[:].unsqueeze(1).to_broadcast([128, KW, 128]), scalar=0.5,
                            in1=colp[:, idx, 0:KW].unsqueeze(2).to_broadcast([128, KW, 128]),
                            op0=ALU.mult, op1=ALU.mult),
                            reads=["ident", ("colp", idx)], writes=[("ring", sl)])
                        S.dma("sp", ring_sem[sl], lambda e, idx=idx, sl=sl: e.dma_start(out=diag_s[idx], in_=ring[:, sl, 0:KW * 128]),
                              reads=[("ring", sl)], writes=[("diag_s", idx)])
                    else:
                        S.dma("sp", ring_sem[sl], lambda e, idx=idx, sl=sl: e.dma_start(out=ring[:, sl, 0:KW * 128], in_=diag_s[idx]),
                              reads=[("diag_s", idx)], writes=[("ring", sl)])
                ws["issued"] += 1

        def release(*seqs):
            for q in seqs:
                ws["free"].append(ws["slot_of"].pop(q))
            prefetch()

        def wslot(seq):
            assert seq in ws["slot_of"], "page %d not resident (ring deadlock)" % seq
            return ws["slot_of"][seq]

        def wpage(seq):
            sl = wslot(seq)
            return ring[:, sl, :].rearrange("p (c n) -> p c n", c=8), ("ring", sl)

        HT = lambda hb: [("hT", hb, s) for s in range(NSUB)]

        def P_front(i, s, hw=False, q="sp"):
            P_load(i, s, hw, q)
            P_chain(i, s)

        def P_load(i, s, hw=False, q="sp"):
            xb = s % 2
            tok0 = i * T + s * 128
            if hw:
                S.dma(q, xinh_sem[xb], lambda e, xb=xb, tok0=tok0: e.dma_start(out=xin[:, xb, :], in_=x_d[tok0:tok0 + 128, :]),
                      writes=[("xin", xb)])
            else:
                S.dma("pool", xin_sem[xb], lambda e, xb=xb, tok0=tok0: e.dma_start(out=xin[:, xb, :], in_=x_d[tok0:tok0 + 128, :]),
                      writes=[("xin", xb)])

        def P_chain(i, s):
            xb = s % 2
            jj = next_junk()
            S.op("act", lambda e, jj=jj, xb=xb: e.activation(out=junk[:, jj, :], in_=xin[:, xb, :], func=AF.Square, accum_out=st[:, xb:xb + 1]),
                 reads=[("xin", xb)], writes=[("junk", jj), ("st", xb)])
            S.op("dve", lambda e, xb=xb: e.tensor_scalar(out=st[:, 2 + xb:3 + xb], in0=st[:, xb:xb + 1], scalar1=1.0 / D, scalar2=EPS,
                                                         op0=ALU.mult, op1=ALU.add), reads=[("st", xb)], writes=[("st", 2 + xb)])
            S.op("pool", lambda e, xb=xb: e.tensor_tensor(out=st[:, 2 + xb:3 + xb], in0=st[:, 2 + xb:3 + xb], in1=nh[:, 0:1], op=ALU.pow),
                 reads=[("st", 2 + xb), "nh"], writes=[("st", 2 + xb)])
            S.op("dve", lambda e, xb=xb, s=s: e.scalar_tensor_tensor(out=xs[:, s, :], in0=xin[:, xb, :], scalar=st[:, 2 + xb:3 + xb], in1=Gn[:],
                                                                op0=ALU.mult, op1=ALU.mult),
                 reads=[("xin", xb), ("st", 2 + xb), "Gn"], writes=[("xs", s)])

        def P_back(i, s):
            hb = i % 2
            b = alloc_bank()
            tpv = bank(b).bitcast(BF16).rearrange("p (a c) -> p a c", a=8)
            for dc in range(8):
                S.op("pe", lambda e, dc=dc, s=s, tpv=tpv: e.transpose(out=tpv[:, dc, :], in_=xs[:, s, dc * 128:(dc + 1) * 128], identity=ident[:]),
                     reads=[("xs", s), "ident"], writes=[PS(b)], inc=(dc == 7))
            S.op("act", lambda e, hb=hb, s=s, tpv=tpv: e.activation(out=hT[:, hb, :, s * 128:(s + 1) * 128], in_=tpv, func=AF.Copy),
                 reads=[PS(b)], writes=[("hT", hb, s)])

        def stage_P(i):
            for s in range(NSUB):
                P_front(i, s)
                P_back(i, s)

        def proj_fm(bk, seq, sub, hb):
            wv, key = wpage(seq)
            for dc in range(8):
                S.op("pe", lambda e, bk=bk, wv=wv, dc=dc, sub=sub, hb=hb: e.matmul(
                    bank(bk), lhsT=wv[:, dc, sub * 128:(sub + 1) * 128], rhs=hT[:, hb, dc, :], start=(dc == 0), stop=(dc == 7)),
                    reads=[key] + HT(hb), writes=[PS(bk)], inc=(dc == 7))

        def A_group(i, g, cur):
            hb = i % 2
            slot = i % 2
            pa_seq = cur + (0 if g < 4 else 2)
            ba = alloc_bank()
            bg = alloc_bank()
            proj_fm(ba, pa_seq, g % 4, hb)
            proj_fm(bg, pa_seq + 1, g % 4, hb)
            j = g % 2
            S.op("act", lambda e, bg=bg, j=j: e.activation(out=th[:, j, :], in_=bank(bg), func=AF.Tanh, scale=0.5),
                 reads=[PS(bg)], writes=[("th", j)])
            S.op("dve", lambda e, ba=ba, j=j, slot=slot, g=g: e.scalar_tensor_tensor(
                out=hg[:, slot, g, 16:528], in0=th[:, j, :], scalar=1.0, in1=bank(ba), op0=ALU.add, op1=ALU.mult),
                reads=[("th", j), PS(ba)], writes=[("hg", slot, g)])
            if g == 3:
                release(cur, cur + 1)
            if g == 7:
                release(cur + 2, cur + 3)

        def A_finish(i):
            slot = i % 2
            ws["cur"] += 4
            if i > 0:
                ps_ = (i - 1) % 2
                S.op("pool", lambda e, ps_=ps_, slot=slot: e.tensor_copy(out=hg[:, ps_, :, 528:544], in_=hg[:, slot, :, 16:32]),
                     reads=[("hg", slot, g) for g in range(8)], writes=[("hgr", ps_)])
                S.op("pool", lambda e, ps_=ps_, slot=slot: e.tensor_copy(out=hg[:, slot, :, 0:16], in_=hg[:, ps_, :, 512:528]),
                     reads=[("hg", ps_, g) for g in range(8)], writes=[("hgl", slot)])
            if i == NT - 1:
                S.op("pool", lambda e, slot=slot: e.memset(hg[:, slot, :, 528:544], 0.0), writes=[("hgr", slot)])

        def stage_A(i):
            cur = ws["cur"]
            for g in range(8):
                A_group(i, g, cur)
            A_finish(i)

        def stage_C(i, head_fn=None, v_fn=None, p_hook=None, early_norm=False):
            hb = i % 2
            slot = i % 2
            cur = ws["cur"]
            bm = alloc_bank()
            pinned.add(bm)
            bq = alloc_bank()
            pinned.add(bq)

            def stats(g):
                S.op("pe", lambda e, g=g: e.matmul(bank(bm), lhsT=onesm[:], rhs=cb[:, g, :], start=(g == 0), stop=(g == 7)),
                     reads=[("cb", g), "onesm"], writes=[PS(bm)], inc=False)
                S.op("pe", lambda e, g=g: e.matmul(bank(bq), lhsT=onesm[:], rhs=sq[:, g % 4, :], start=(g == 0), stop=(g == 7)),
                     reads=[("sq", g % 4), "onesm"], writes=[PS(bq)], inc=True)

            for g in range(8):
                seq = cur + DIAG_OFF[g]
                sl = wslot(seq)
                dv = ring[:, sl, 0:KW * 128].rearrange("p (k c) -> p k c", k=KW)
                bc = alloc_bank()
                pe_taps = list(range(NDVE, KW))
                for k in pe_taps:
                    S.op("pe", lambda e, bc=bc, dv=dv, k=k, g=g: e.matmul(
                        bank(bc), lhsT=dv[:, k, :], rhs=hg[:, slot, g, 1 + k:1 + k + 512], start=(k == pe_taps[0]), stop=(k == KW - 1)),
                        reads=[("ring", sl), ("hg", slot, g), ("hgl", slot), ("hgr", slot)], writes=[PS(bc)], inc=(k == KW - 1))
                release(seq)
                ja = g % 2
                for k in range(NDVE):
                    if k == 0:
                        S.op("dve", lambda e, k=k, g=g, ja=ja: e.tensor_scalar(
                            out=t1[:, ja, :], in0=hg[:, slot, g, 1 + k:1 + k + 512], scalar1=colh[:, g, k:k + 1], scalar2=None, op0=ALU.mult),
                            reads=[("hg", slot, g), ("hgl", slot), ("hgr", slot), ("colp", g)], writes=[("t1", ja)])
                    else:
                        S.op("dve", lambda e, k=k, g=g, ja=ja: e.scalar_tensor_tensor(
                            out=t1[:, ja, :], in0=hg[:, slot, g, 1 + k:1 + k + 512], scalar=colh[:, g, k:k + 1], in1=t1[:, ja, :],
                            op0=ALU.mult, op1=ALU.add),
                            reads=[("hg", slot, g), ("hgl", slot), ("hgr", slot), ("t1", ja), ("colp", g)], writes=[("t1", ja)])
                if NDVE > 0:
                    S.op("dve", lambda e, bc=bc, g=g, ja=ja: e.scalar_tensor_tensor(
                        out=cb[:, g, :], in0=bank(bc), scalar=colp[:, g, 31:32], in1=t1[:, ja, :], op0=ALU.add, op1=ALU.add),
                        reads=[PS(bc), ("colp", g), ("t1", ja)], writes=[("cb", g)])
                else:
                    S.op("act", lambda e, bc=bc, g=g: e.activation(out=cb[:, g, :], in_=bank(bc), func=AF.Identity, bias=colp[:, g, 31:32]),
                         reads=[PS(bc), ("colp", g)], writes=[("cb", g)])
                S.op("pool", lambda e, g=g: e.tensor_tensor(out=sq[:, g % 4, :], in0=cb[:, g, :], in1=cb[:, g, :], op=ALU.mult),
                     reads=[("cb", g)], writes=[("sq", g % 4)])
                if g >= STATS_LAG:
                    stats(g - STATS_LAG)
                if head_fn is not None:
                    head_fn(g)
                if deferred and g < 2:
                    deferred.pop(0)()
                if p_hook is not None:
                    p_hook(g)
            for g_ in range(8 - STATS_LAG, 8):
                stats(g_)
            def Gp(g):
                bgc = alloc_bank()
                proj_fm(bgc, cur + 12 + g // 4, g % 4, hb)
                if g == 3:
                    release(cur + 12)
                if g == 7:
                    release(cur + 13)
                S.op("act", lambda e, bgc=bgc, g=g: e.activation(out=sgc[:, g, :], in_=bank(bgc), func=AF.Silu),
                     reads=[PS(bgc)], writes=[("sgc", g)])

            def postproc():
                S.op("act", lambda e: e.activation(out=mean_sb[:], in_=bank(bm), func=AF.Copy), reads=[PS(bm)], writes=["mean_sb"])
                S.op("dve", lambda e: e.scalar_tensor_tensor(out=nmr_sb[:], in0=mean_sb[:], scalar=-1.0, in1=mean_sb[:], op0=ALU.mult, op1=ALU.mult),
                     reads=["mean_sb"], writes=["nmr_sb"])
                S.op("dve", lambda e: e.tensor_tensor(out=rstd_sb[:], in0=bank(bq), in1=nmr_sb[:], op=ALU.add),
                     reads=[PS(bq), "nmr_sb"], writes=["rstd_sb"])
                pinned.discard(bm)
                pinned.discard(bq)
                S.op("dve", lambda e: e.tensor_scalar(out=rstd_sb[:], in0=rstd_sb[:], scalar1=0.0, scalar2=EPS, op0=ALU.max, op1=ALU.add),
                     reads=["rstd_sb"], writes=["rstd_sb"])
                S.op("act", lambda e: e.activation(out=rstd_sb[:], in_=rstd_sb[:], func=AF.Sqrt), reads=["rstd_sb"], writes=["rstd_sb"])
                S.op("dve", lambda e: e.reciprocal(out=rstd_sb[:], in_=rstd_sb[:]), reads=["rstd_sb"], writes=["rstd_sb"])
                S.op("dve", lambda e: e.scalar_tensor_tensor(out=nmr_sb[:], in0=mean_sb[:], scalar=-1.0, in1=rstd_sb[:], op0=ALU.mult, op1=ALU.mult),
                     reads=["mean_sb", "rstd_sb"], writes=["nmr_sb"])

            if early_norm:
                postproc()

                def Gn_(g):
                    Gp(g)
                    norm_group(g)
                v_fn(Gn_)
            else:
                v_fn(Gp)
                postproc()
            ws["cur"] += 16

        def norm_a(g):
            j = g % 2
            S.op("dve", lambda e, g=g, j=j: e.tensor_tensor(out=t1[:, j, :], in0=cb[:, g, :], in1=rstd_sb[:], op=ALU.mult),
                 reads=[("cb", g), "rstd_sb"], writes=[("t1", j)])
            S.op("dve", lambda e, j=j: e.tensor_tensor(out=t1[:, j, :], in0=t1[:, j, :], in1=nmr_sb[:], op=ALU.add),
                 reads=[("t1", j), "nmr_sb"], writes=[("t1", j)])
            S.op("act", lambda e, g=g, j=j: e.activation(out=sw[:, j, :], in_=t1[:, j, :], func=AF.Silu,
                                                         scale=colp[:, g, 32:33], bias=colp[:, g, 33:34]),
                 reads=[("t1", j), ("colp", g)], writes=[("sw", j)])

        def norm_b(g):
            j = g % 2
            S.op("dve", lambda e, g=g, j=j: e.tensor_tensor(out=ycv[:, g, :], in0=sw[:, j, :], in1=sgc[:, g, :], op=ALU.mult),
                 reads=[("sw", j), ("sgc", g)], writes=[("ycv", g)])

        def norm_group(g):
            norm_a(g)
            norm_b(g)

        def S_head(i, h, cur):
            hb = i % 2
            pu_seq = cur + (0 if h < 4 else 7)
            bu = alloc_bank()
            bgs = alloc_bank()
            proj_fm(bu, pu_seq, h % 4, hb)
            proj_fm(bgs, pu_seq + 1, h % 4, hb)
            if h == 3:
                release(cur + 0, cur + 1)
            if h == 7:
                release(cur + 7, cur + 8)
            j = h % 2
            YK = [("ysg", h, s) for s in range(NSUB)]
            S.op("act", lambda e, bgs=bgs, j=j: e.activation(out=sgs[:, j, :], in_=bank(bgs), func=AF.Silu),
                 reads=[PS(bgs)], writes=[("sgs", j)])
            S.op("dve", lambda e, bu=bu, j=j, h=h: e.tensor_tensor(out=ysg[:, h, :], in0=bank(bu), in1=sgs[:, j, :], op=ALU.mult),
                 reads=[PS(bu), ("sgs", j)], writes=YK)

        def stage_V(i, Gp, Pb):
            hb = i % 2
            cur = ws["cur"]
            wv0, k0 = wpage(cur + 14)
            wv1, k1 = wpage(cur + 15)

            def pv(s):
                bv = alloc_pair()
                for dc in range(8):
                    for half in range(2):
                        wv, key = (wv0, k0) if half == 0 else (wv1, k1)
                        S.op("pe", lambda e, bv=bv, dc=dc, half=half, wv=wv, s=s: e.matmul(
                            bank(bv + half), lhsT=hT[:, hb, dc, s * 128:(s + 1) * 128], rhs=wv[:, dc, :], start=(dc == 0), stop=(dc == 7)),
                            reads=[("hT", hb, s), key], writes=[PS(bv + half)], inc=(dc == 7 and half == 1))
                vj = s % 2
                pvv = pp[bv // 2]
                PV = [PS(bv), PS(bv + 1)]
                jj = next_junk()
                S.op("act", lambda e, jj=jj, pvv=pvv, vj=vj: e.activation(out=junk[:, jj, :], in_=pvv[:, :], func=AF.Identity, accum_out=st[:, 4 + vj:5 + vj]),
                     reads=PV, writes=[("junk", jj), ("st", 4 + vj)])
                jj = next_junk()
                S.op("act", lambda e, jj=jj, pvv=pvv, vj=vj: e.activation(out=junk[:, jj, :], in_=pvv[:, :], func=AF.Square, accum_out=st[:, 6 + vj:7 + vj]),
                     reads=PV, writes=[("junk", jj), ("st", 6 + vj)])
                S.op("dve", lambda e, vj=vj: e.tensor_scalar(out=st[:, 8 + vj:9 + vj], in0=st[:, 4 + vj:5 + vj], scalar1=1.0 / D, scalar2=None, op0=ALU.mult),
                     reads=[("st", 4 + vj)], writes=[("st", 8 + vj)])
                S.op("dve", lambda e, pvv=pvv, vj=vj: e.scalar_tensor_tensor(out=vg[:, vj, :], in0=pvv[:, :], scalar=st[:, 8 + vj:9 + vj], in1=Gv[:],
                                                                             op0=ALU.subtract, op1=ALU.mult),
                     reads=PV + [("st", 8 + vj), "Gv"], writes=[("vg", vj)])
                S.op("dve", lambda e, vj=vj: e.scalar_tensor_tensor(out=st[:, 10 + vj:11 + vj], in0=st[:, 8 + vj:9 + vj], scalar=-1.0, in1=st[:, 8 + vj:9 + vj],
                                                                    op0=ALU.mult, op1=ALU.mult),
                     reads=[("st", 8 + vj)], writes=[("st", 10 + vj)])
                S.op("dve", lambda e, vj=vj: e.scalar_tensor_tensor(out=st[:, 12 + vj:13 + vj], in0=st[:, 6 + vj:7 + vj], scalar=1.0 / D, in1=st[:, 10 + vj:11 + vj],
                                                                    op0=ALU.mult, op1=ALU.add),
                     reads=[("st", 6 + vj), ("st", 10 + vj)], writes=[("st", 12 + vj)])
                S.op("dve", lambda e, vj=vj: e.tensor_scalar(out=st[:, 12 + vj:13 + vj], in0=st[:, 12 + vj:13 + vj], scalar1=0.0, scalar2=EPS,
                                                             op0=ALU.max, op1=ALU.add),
                     reads=[("st", 12 + vj)], writes=[("st", 12 + vj)])
                S.op("pool", lambda e, vj=vj: e.tensor_tensor(out=st[:, 12 + vj:13 + vj], in0=st[:, 12 + vj:13 + vj], in1=nh[:, 0:1], op=ALU.pow),
                     reads=[("st", 12 + vj), "nh"], writes=[("st", 12 + vj)])
                S.op("dve", lambda e, vj=vj: e.tensor_scalar(out=vg[:, vj, :], in0=vg[:, vj, :], scalar1=st[:, 12 + vj:13 + vj], scalar2=None, op0=ALU.mult),
                     reads=[("vg", vj), ("st", 12 + vj)], writes=[("vg", vj)])

            def mix(s):
                vj = s % 2
                bm2 = alloc_pair()
                pm = pp[bm2 // 2]
                for h in range(8):
                    S.op("pe", lambda e, pm=pm, h=h, vj=vj: e.matmul(
                        pm[:, h * 128:(h + 1) * 128], lhsT=vg[:, vj, h * 128:(h + 1) * 128], rhs=wst[:, h * 128:(h + 1) * 128],
                        start=True, stop=False, skip_group_check=True),
                        reads=[("vg", vj), "wst"], writes=[PS(bm2), PS(bm2 + 1)], inc=False)
                    S.op("pe", lambda e, pm=pm, h=h: e.matmul(
                        pm[:, h * 128:(h + 1) * 128], lhsT=lsm[:, h, :], rhs=rsm[:, h, :], start=False, stop=True, skip_group_check=True),
                        reads=["lsm"] + RSM, writes=[PS(bm2), PS(bm2 + 1)], inc=(h == 7))
                S.op("dve", lambda e, pm=pm, s=s: e.tensor_tensor(
                    out=ysg[:, :, s * 128:(s + 1) * 128], in0=pm[:, :].rearrange("p (h c) -> p h c", h=8),
                    in1=ysg[:, :, s * 128:(s + 1) * 128], op=ALU.mult),
                    reads=[PS(bm2), PS(bm2 + 1)] + [("ysg", h, s) for h in range(8)], writes=[("ysg", h, s) for h in range(8)])

            Gp(0)
            Gp(1)
            Gp(2)
            Gp(3)
            pv(0)
            Gp(4)
            Gp(5)
            pv(1)
            mix(0)
            Gp(6)
            Gp(7)
            pv(2)
            mix(1)
            pv(3)
            release(cur + 14, cur + 15)
            Pb(0)
            Pb(1)
            mix(2)
            Pb(2)
            Pb(3)
            mix(3)

        ostate = {}

        def O_mm(i, s, ecs, first, last):
            bo = ostate[s]
            for ec in ecs:
                for half in range(2):
                    if ec < 8:
                        lt = ycv[:, ec, s * 128:(s + 1) * 128]
                        rk = ("ycv", ec)
                    else:
                        lt = ysg[:, ec - 8, s * 128:(s + 1) * 128]
                        rk = ("ysg", ec - 8, s)
                    S.op("pe", lambda e, bo=bo, half=half, lt=lt, ec=ec: e.matmul(
                        bank(bo + half), lhsT=lt, rhs=wout[:, ec, half * 512:(half + 1) * 512],
                        start=(first and ec == ecs[0]), stop=(last and ec == ecs[-1])),
                        reads=[rk, ("wout", ec // 4)], writes=[PS(bo + half)], inc=(last and ec == ecs[-1] and half == 1))

        def O_load(i, s):
            r = (i * NSUB + s) % 3
            tok0 = i * T + s * 128
            S.dma("pool", xres_sem[r], lambda e, r=r, tok0=tok0: e.dma_start(out=xres[:, r, :], in_=x_d[tok0:tok0 + 128, :]),
                  writes=[("xres", r)])

        def O_a1(i, s, load=True):
            if load:
                O_load(i, s)
            ostate[s] = alloc_pair()
            O_mm(i, s, list(range(8, 16)), True, False)

        def O_a2(i, s):
            r = (i * NSUB + s) % 3
            bo = ostate[s]
            O_mm(i, s, list(range(8)), False, True)
            po = pp[bo // 2]
            S.op("dve", lambda e, po=po, r=r: e.tensor_tensor(out=xres[:, r, :], in0=po[:, :], in1=xres[:, r, :], op=ALU.add),
                 reads=[PS(bo), PS(bo + 1), ("xres", r)], writes=[("xres", r)])
            jj = next_junk()
            S.op("act", lambda e, jj=jj, r=r: e.activation(out=junk[:, jj, :], in_=xres[:, r, :], func=AF.Square, accum_out=st[:, 16 + r:17 + r]),
                 reads=[("xres", r)], writes=[("junk", jj), ("st", 16 + r)])

        def O_a(i, s):
            O_a1(i, s)
            O_a2(i, s)

        def O_b(i, s):
            r = (i * NSUB + s) % 3
            tok0 = i * T + s * 128
            S.op("dve", lambda e, r=r: e.tensor_scalar(out=st[:, 20 + r:21 + r], in0=st[:, 16 + r:17 + r], scalar1=1.0 / D, scalar2=EPS,
                                                       op0=ALU.mult, op1=ALU.add), reads=[("st", 16 + r)], writes=[("st", 20 + r)])
            S.op("pool", lambda e, r=r: e.tensor_tensor(out=st[:, 20 + r:21 + r], in0=st[:, 20 + r:21 + r], in1=nh[:, 0:1], op=ALU.pow),
                 reads=[("st", 20 + r), "nh"], writes=[("st", 20 + r)])
            S.op("dve", lambda e, r=r: e.scalar_tensor_tensor(out=xres[:, r, :], in0=xres[:, r, :], scalar=st[:, 20 + r:21 + r], in1=Gfin[:],
                                                              op0=ALU.mult, op1=ALU.mult),
                 reads=[("xres", r), ("st", 20 + r), "Gfin"], writes=[("xres", r)])
            S.dma("sp", out_sem[r], lambda e, r=r, tok0=tok0: e.dma_start(out=out_d[tok0:tok0 + 128, :], in_=xres[:, r, :]),
                  reads=[("xres", r)])

        deferred = []

        def stage_O_all(i):
            O_a1(i, 0)
            O_a1(i, 1)
            O_a2(i, 0)
            O_a2(i, 1)
            O_b(i, 0)
            O_a(i, 2)
            O_b(i, 1)
            O_a(i, 3)
            deferred.append(lambda i=i: O_b(i, 2))
            deferred.append(lambda i=i: O_b(i, 3))

        P_load(0, 0, hw=True)
        P_load(0, 1, hw=True)
        gn_sem = S.dma_sem("gn")
        S.dma("sp", gn_sem, lambda e: e.dma_start(out=Gn[:], in_=norm_g_d[0:1, :].to_broadcast([128, 1024])), writes=["Gn"])
        prologue_params()
        prefetch()
        P_chain(0, 0)
        P_load(0, 2, hw=True, q="act")
        P_chain(0, 1)
        P_load(0, 3, hw=True, q="act")
        P_back(0, 0)
        P_back(0, 1)
        P_chain(0, 2)
        P_chain(0, 3)
        P_back(0, 2)
        P_back(0, 3)
        if NT > 1:
            for s_ in range(NSUB):
                P_front(1, s_, hw=True, q="act")
        cur0 = ws["cur"]
        for g in range(8):
            A_group(0, g, cur0)
            if NT > 1 and g % 2 == 1:
                P_back(1, g // 2)
        A_finish(0)
        wsems = [S.dma_sem("wo%d" % i) for i in range(4)]
        for q in range(4):
            S.dma("pool", wsems[q], lambda e, q=q: e.dma_start(
                out=wout[:, q * 4:(q + 1) * 4, :],
                in_=w_out_d[q * 512:(q + 1) * 512, :].rearrange("(c p) d -> p c d", p=128)), writes=[("wout", q)])
        WOUT = [("wout", q) for q in range(4)]

        if NT > 1:
            stage_A(1)
        for i in range(NT):
            nxt = i + 2 < NT
            cur = ws["cur"]

            def vfn(Gp, i=i, nxt=nxt):
                stage_V(i, Gp, (lambda s_: P_back(i + 2, s_)) if nxt else (lambda s_: None))

            def phook(g, i=i, nxt=nxt):
                if not nxt or g % 2 == 0:
                    return
                s_ = g // 2
                P_chain(i + 2, s_)
                if s_ + 2 < NSUB:
                    P_load(i + 2, s_ + 2)

            if nxt:
                P_load(i + 2, 0)
                P_load(i + 2, 1)
            stage_C(i, head_fn=lambda h, i=i, cur=cur: S_head(i, h, cur), v_fn=vfn, p_hook=phook, early_norm=False)
            if nxt:
                cura = ws["cur"]
                A_group(i + 2, 0, cura)
                for g in range(8):
                    if g + 1 < 8:
                        A_group(i + 2, g + 1, cura)
                    norm_a(g)
                    if g > 0:
                        norm_b(g - 1)
                norm_b(7)
                A_finish(i + 2)
            if nxt:
                stage_O_all(i)
            else:
                for s_ in range(NSUB):
                    O_a1(i, s_, load=False)
                    norm_a(2 * s_)
                    if s_ > 0:
                        norm_b(2 * s_ - 1)
                    norm_a(2 * s_ + 1)
                    norm_b(2 * s_)
                norm_b(7)
                O_load(i, 0)
                O_load(i, 1)
                O_a2(i, 0)
                O_load(i, 2)
                O_a2(i, 1)
                O_b(i, 0)
                O_a2(i, 2)
                O_b(i, 1)
                O_load(i, 3)
                O_a2(i, 3)
                deferred.append(lambda i=i: O_b(i, 2))
                deferred.append(lambda i=i: O_b(i, 3))
        while deferred:
            deferred.pop(0)()
        print("sbuf bytes remaining:", nc.sbuf_bytes_remaining)
        S.finish(["sp"], out_sem)
        S.emit()
    return nc


_PROG_CACHE = {}


def _run(x, params, n_cores):
    S_TOK = x.shape[1]
    if S_TOK not in _PROG_CACHE:
        _PROG_CACHE[S_TOK] = build_program(S_TOK)
    nc = _PROG_CACHE[S_TOK]
    in_maps = []
    for c in range(n_cores):
        m = dict(params)
        m["x"] = np.ascontiguousarray(x[c])
        in_maps.append(m)
    res = run_bass_kernel_spmd(nc, in_maps, core_ids=list(range(n_cores)))
    return np.stack([r["out"] for r in res.results], axis=0)


def _params(norm_g, w_in, conv_w, conv_b, conv_ln_g, conv_ln_b, sgu_ln_g, sgu_ln_b, w_s, b_s, w_out, final_g):
    f = lambda a: np.ascontiguousarray(np.asarray(a, dtype=np.float32))
    return {
        "w_in": f(w_in[0]), "w_out": f(w_out[0]), "conv_w": f(conv_w[0]),
        "conv_b": f(conv_b[0]).reshape(1, D), "conv_ln_g": f(conv_ln_g[0]).reshape(1, D),
        "conv_ln_b": f(conv_ln_b[0]).reshape(1, D), "sgu_ln_g": f(sgu_ln_g[0]).reshape(1, D),
        "sgu_ln_b": f(sgu_ln_b[0]).reshape(1, D), "norm_g": f(norm_g[0]).reshape(1, D),
        "final_g": f(final_g).reshape(1, D), "w_s": f(w_s[0]), "b_s": f(b_s[0]).reshape(1, D),
    }


def kernel(x, norm_g, w_in, conv_w, conv_b, conv_ln_g, conv_ln_b, sgu_ln_g, sgu_ln_b, w_s, b_s, w_out, final_g):
    x = np.asarray(x, dtype=np.float32)
    params = _params(norm_g, w_in, conv_w, conv_b, conv_ln_g, conv_ln_b, sgu_ln_g, sgu_ln_b, w_s, b_s, w_out, final_g)
    out = _run(x, params, x.shape[0])
    return out.astype(np.float32)
```
